# Optimizing a Trainium2 kernel written in Bass

```python
import math
import jax, jax.numpy as jnp
from jax import lax
import numpy as np

D_MODEL = 1024
BATCH = 4
SEQ = 8192
DEPTH = 2

CHUNK = 64
SB_BLOCK = 128
RET_HEADS = 4
RET_DIM = 128
SB_HEADS = 4
SB_DIM = 128
SSM_HEADS = 8
SSM_HEAD_DIM = 64
SSM_STATE = 128
SSM_GROUPS = 2
SSM_CONV = 4
D_FF = 2816
ROPE_BASE = 10000.0
NORM_EPS = 1e-6
N_SUB = 3

RET_W = RET_HEADS * RET_DIM
SB_W = SB_HEADS * SB_DIM
SSM_W = SSM_HEADS * SSM_HEAD_DIM
MIX_W = RET_W + SB_W + SSM_W
SSM_XBC = SSM_W + 2 * SSM_GROUPS * SSM_STATE
IN_W = 4 * RET_W + 3 * SB_W + SSM_W + SSM_XBC + SSM_HEADS

kernel_name = "hybrid_ret_sb_ssd_macaron_adaln"


def rmsnorm(x, gain):
    xf = x.astype(jnp.float32)
    y = xf * lax.rsqrt(jnp.mean(xf * xf, axis=-1, keepdims=True) + NORM_EPS)
    return (y * gain.astype(jnp.float32)).astype(x.dtype)


def modulate(h, shift, scale):
    return h * (1.0 + scale) + shift


def swiglu(u, wg, wu, wd):
    return (jax.nn.silu(u @ wg) * (u @ wu)) @ wd


def rope(x, pos):
    half = x.shape[-1] // 2
    inv_freq = ROPE_BASE ** (-jnp.arange(half, dtype=jnp.float32) / half)
    ang = pos[:, None] * inv_freq[None, :]
    cos = jnp.cos(ang)[None, :, None, :]
    sin = jnp.sin(ang)[None, :, None, :]
    x1 = x[..., :half].astype(jnp.float32)
    x2 = x[..., half:].astype(jnp.float32)
    return jnp.concatenate([x1 * cos - x2 * sin, x1 * sin + x2 * cos], axis=-1)


def retention(q, k, v, g, gn_gain):
    B, S, _ = q.shape
    nc = S // CHUNK
    H, Dh = RET_HEADS, RET_DIM
    pos = jnp.arange(S, dtype=jnp.float32)
    q = rope(q.reshape(B, S, H, Dh), pos)
    k = rope(k.reshape(B, S, H, Dh), pos) * (Dh ** -0.5)
    v = v.reshape(B, S, H, Dh).astype(jnp.float32)
    log_gamma = jnp.log1p(-(2.0 ** (-5.0 - jnp.arange(H, dtype=jnp.float32))))
    idx = jnp.arange(CHUNK, dtype=jnp.float32)
    dmat = jnp.exp(log_gamma[:, None, None] * jnp.abs(idx[:, None] - idx[None, :]))
    qc = q.reshape(B, nc, CHUNK, H, Dh)
    kc = k.reshape(B, nc, CHUNK, H, Dh)
    vc = v.reshape(B, nc, CHUNK, H, Dh)
    scores = jnp.einsum('bclhd,bcshd->bchls', qc, kc) * dmat[None, None]
    y_intra = jnp.einsum('bchls,bcshe->bclhe', scores, vc)
    k_decay = jnp.exp(log_gamma[:, None] * (CHUNK - 1 - idx)[None, :])
    kv = jnp.einsum('bcshd,hs,bcshe->bchde', kc, k_decay, vc).astype(jnp.float32)
    chunk_decay = jnp.exp(log_gamma * CHUNK)[None, :, None, None]

    def step(state, kv_c):
        return state * chunk_decay + kv_c, state

    _, s_prev = lax.scan(step, jnp.zeros((B, H, Dh, Dh), jnp.float32), jnp.moveaxis(kv, 1, 0))
    s_prev = jnp.moveaxis(s_prev, 0, 1)
    q_decay = jnp.exp(log_gamma[:, None] * (idx + 1.0)[None, :])
    y_cross = jnp.einsum('bclhd,hl,bchde->bclhe', qc, q_decay, s_prev)
    y = (y_intra + y_cross).reshape(B, S, H, Dh)
    y = rmsnorm(y, gn_gain.reshape(H, Dh)).reshape(B, S, RET_W)
    return y * jax.nn.silu(g.astype(jnp.float32))


def stick_breaking(q, k, v):
    B, S, _ = q.shape
    H, Dh = SB_HEADS, SB_DIM
    q = q.reshape(B, S, H, Dh).transpose(0, 2, 1, 3)
    k = k.reshape(B, S, H, Dh).transpose(0, 2, 1, 3)
    v = v.reshape(B, S, H, Dh).transpose(0, 2, 1, 3)
    scale = Dh ** -0.5
    outs = []
    for i in range(S // SB_BLOCK):
        q0 = i * SB_BLOCK
        kend = q0 + SB_BLOCK
        qb = q[:, :, q0:kend]
        kb = k[:, :, :kend]
        vb = v[:, :, :kend]
        z = jnp.einsum('bhtd,bhsd->bhts', qb, kb).astype(jnp.float32) * scale
        t_pos = q0 + jnp.arange(SB_BLOCK)
        s_pos = jnp.arange(kend)
        visible = s_pos[None, :] < t_pos[:, None]
        log_beta = jax.nn.log_sigmoid(z)
        log_keep = jnp.where(visible, jax.nn.log_sigmoid(-z), 0.0)
        tail = lax.cumsum(log_keep, axis=3, reverse=True) - log_keep
        w = jnp.where(visible, jnp.exp(log_beta + tail), 0.0)
        outs.append(jnp.einsum('bhts,bhsd->bhtd', w, vb.astype(jnp.float32)))
    y = jnp.concatenate(outs, axis=2)
    return y.transpose(0, 2, 1, 3).reshape(B, S, SB_W)


def mamba2(z, xbc, dt_raw, conv_w, conv_b, dt_bias, a_log, d_skip, norm_gain):
    B, S, _ = xbc.shape
    G, Hg, P, N = SSM_GROUPS, SSM_HEADS // SSM_GROUPS, SSM_HEAD_DIM, SSM_STATE
    nc = S // CHUNK
    xbc = lax.conv_general_dilated(
        xbc, conv_w[:, None, :], window_strides=(1,), padding=[(SSM_CONV - 1, 0)],
        dimension_numbers=('NWC', 'WIO', 'NWC'), feature_group_count=SSM_XBC) + conv_b
    xbc = jax.nn.silu(xbc).astype(jnp.float32)
    xs = xbc[..., :SSM_W].reshape(B, nc, CHUNK, G, Hg, P)
    bm = xbc[..., SSM_W:SSM_W + G * N].reshape(B, nc, CHUNK, G, N)
    cm = xbc[..., SSM_W + G * N:].reshape(B, nc, CHUNK, G, N)
    dt = jax.nn.softplus(dt_raw.astype(jnp.float32) + dt_bias).reshape(B, nc, CHUNK, G, Hg)
    a = -jnp.exp(a_log.astype(jnp.float32)).reshape(G, Hg)
    acum = jnp.cumsum(dt * a, axis=2)
    xdt = xs * dt[..., None]
    causal = jnp.tril(jnp.ones((CHUNK, CHUNK), dtype=bool))[None, None, :, :, None, None]
    seg = acum[:, :, :, None] - acum[:, :, None, :]
    decay = jnp.exp(jnp.where(causal, seg, -jnp.inf))
    cb = jnp.einsum('bclgn,bcsgn->bcgls', cm, bm)
    y_intra = jnp.einsum('bcgls,bclsgh,bcsghp->bclghp', cb, decay, xdt)
    decay_end = jnp.exp(acum[:, :, -1:] - acum)
    states = jnp.einsum('bcsgn,bcsgh,bcsghp->bcghpn', bm, decay_end, xdt)
    chunk_decay = jnp.exp(acum[:, :, -1])

    def step(h, inp):
        st, dec = inp
        return h * dec[..., None, None] + st, h

    _, h_prev = lax.scan(step, jnp.zeros((B, G, Hg, P, N), jnp.float32),
                         (jnp.moveaxis(states, 1, 0), jnp.moveaxis(chunk_decay, 1, 0)))
    h_prev = jnp.moveaxis(h_prev, 0, 1)
    y_inter = jnp.einsum('bclgn,bcghpn,bclgh->bclghp', cm, h_prev, jnp.exp(acum))
    y = y_intra + y_inter + xs * d_skip.astype(jnp.float32).reshape(G, Hg)[..., None]
    y = y.reshape(B, S, SSM_W)
    return rmsnorm(y * jax.nn.silu(z.astype(jnp.float32)), norm_gain)


def setup_inputs(seed: int = 0) -> dict:
    key = jax.random.key(seed)
    ks = jax.random.split(key, 32)
    f32 = jnp.float32
    nrm = lambda k, shape, s: jax.random.normal(k, shape, f32) * s
    gain = lambda k, shape: 1.0 + 0.01 * jax.random.normal(k, shape, f32)
    dt0 = jnp.exp(jax.random.uniform(ks[11], (DEPTH, SSM_HEADS), f32, math.log(1e-3), math.log(1e-1)))
    return {
        "x": nrm(ks[0], (BATCH, SEQ, D_MODEL), 1.0),
        "c": nrm(ks[1], (BATCH, D_MODEL), 1.0),
        "ada_w": nrm(ks[2], (DEPTH, D_MODEL, 3 * N_SUB * D_MODEL), 0.1 * D_MODEL ** -0.5),
        "ada_b": nrm(ks[3], (DEPTH, 3 * N_SUB * D_MODEL), 0.01),
        "norm_ffn1": gain(ks[4], (DEPTH, D_MODEL)),
        "ffn1_wg": nrm(ks[5], (DEPTH, D_MODEL, D_FF), D_MODEL ** -0.5),
        "ffn1_wu": nrm(ks[6], (DEPTH, D_MODEL, D_FF), D_MODEL ** -0.5),
        "ffn1_wd": nrm(ks[7], (DEPTH, D_FF, D_MODEL), D_FF ** -0.5),
        "norm_mix": gain(ks[8], (DEPTH, D_MODEL)),
        "w_in": nrm(ks[9], (DEPTH, D_MODEL, IN_W), D_MODEL ** -0.5),
        "conv_w": nrm(ks[10], (DEPTH, SSM_CONV, SSM_XBC), SSM_CONV ** -0.5),
        "conv_b": nrm(ks[12], (DEPTH, SSM_XBC), 0.01),
        "dt_bias": dt0 + jnp.log(-jnp.expm1(-dt0)),
        "a_log": jnp.log(jax.random.uniform(ks[13], (DEPTH, SSM_HEADS), f32, 1.0, 16.0)),
        "d_skip": gain(ks[14], (DEPTH, SSM_HEADS)),
        "ret_gn": gain(ks[15], (DEPTH, RET_W)),
        "ssm_norm": gain(ks[16], (DEPTH, SSM_W)),
        "w_out": nrm(ks[17], (DEPTH, MIX_W, D_MODEL), MIX_W ** -0.5),
        "norm_ffn2": gain(ks[18], (DEPTH, D_MODEL)),
        "ffn2_wg": nrm(ks[19], (DEPTH, D_MODEL, D_FF), D_MODEL ** -0.5),
        "ffn2_wu": nrm(ks[20], (DEPTH, D_MODEL, D_FF), D_MODEL ** -0.5),
        "ffn2_wd": nrm(ks[21], (DEPTH, D_FF, D_MODEL), D_FF ** -0.5),
        "final_ada_w": nrm(ks[22], (D_MODEL, 2 * D_MODEL), 0.1 * D_MODEL ** -0.5),
        "final_ada_b": nrm(ks[23], (2 * D_MODEL,), 0.01),
        "final_norm": gain(ks[24], (D_MODEL,)),
    }


def reference(x, c, ada_w, ada_b, norm_ffn1, ffn1_wg, ffn1_wu, ffn1_wd, norm_mix, w_in,
              conv_w, conv_b, dt_bias, a_log, d_skip, ret_gn, ssm_norm, w_out,
              norm_ffn2, ffn2_wg, ffn2_wu, ffn2_wd, final_ada_w, final_ada_b, final_norm):
    B, S, _ = x.shape
    cond = jax.nn.silu(c)
    splits = [RET_W, 2 * RET_W, 3 * RET_W, 4 * RET_W,
              4 * RET_W + SB_W, 4 * RET_W + 2 * SB_W, 4 * RET_W + 3 * SB_W,
              4 * RET_W + 3 * SB_W + SSM_W, 4 * RET_W + 3 * SB_W + SSM_W + SSM_XBC]
    h = x
    for l in range(DEPTH):
        mod = (cond @ ada_w[l] + ada_b[l]).reshape(B, 3 * N_SUB, D_MODEL)[:, None]
        u = modulate(rmsnorm(h, norm_ffn1[l]), mod[:, :, 0], mod[:, :, 1])
        h = h + 0.5 * (1.0 + mod[:, :, 2]) * swiglu(u, ffn1_wg[l], ffn1_wu[l], ffn1_wd[l])
        u = modulate(rmsnorm(h, norm_mix[l]), mod[:, :, 3], mod[:, :, 4])
        proj = u @ w_in[l]
        rq, rk, rv, rg, sq, sk, sv, mz, mxbc, mdt = jnp.split(proj, splits, axis=-1)
        y_ret = retention(rq, rk, rv, rg, ret_gn[l]).astype(x.dtype)
        y_sb = stick_breaking(sq, sk, sv).astype(x.dtype)
        y_ssm = mamba2(mz, mxbc, mdt, conv_w[l], conv_b[l], dt_bias[l], a_log[l],
                       d_skip[l], ssm_norm[l]).astype(x.dtype)
        mixed = jnp.concatenate([y_ret, y_sb, y_ssm], axis=-1) @ w_out[l]
        h = h + (1.0 + mod[:, :, 5]) * mixed
        u = modulate(rmsnorm(h, norm_ffn2[l]), mod[:, :, 6], mod[:, :, 7])
        h = h + 0.5 * (1.0 + mod[:, :, 8]) * swiglu(u, ffn2_wg[l], ffn2_wu[l], ffn2_wd[l])
    fmod = (cond @ final_ada_w + final_ada_b).reshape(B, 2, D_MODEL)[:, None]
    return modulate(rmsnorm(h, final_norm), fmod[:, :, 0], fmod[:, :, 1])
```

```python
import contextlib
import math
import numpy as np
import concourse.bass as bass
import concourse.mybir as mybir
from concourse.bass_utils import run_bass_kernel_spmd

F32 = mybir.dt.float32
F32R = mybir.dt.float32r
BF16 = mybir.dt.bfloat16
ALU = mybir.AluOpType
AF = mybir.ActivationFunctionType

D = 1024
NB = 4
SEQ = 8192
NTOK = 4096
DFF = 2816
NFC = 22
EPS = 1e-6
INW = 5128
ENGS = ("pe", "act", "dve", "pool", "sp")
DT_SIZE = {F32: 4, F32R: 4, BF16: 2}


class Buf:
    __slots__ = ("name", "t", "last_w", "readers", "excl", "last_acc")

    def __init__(self, name, t, excl=False):
        self.name = name
        self.t = t
        self.last_w = None
        self.readers = []
        self.excl = excl
        self.last_acc = []

    def __getitem__(self, k):
        return self.t[k]


class Op:
    __slots__ = ("idx", "eng", "fn", "deps", "is_dma", "chan", "n_dma", "val",
                 "needs_inc", "barrier", "bvals", "desc", "unit")


class Sched:
    def __init__(self, nc, es, arena_f32=52000):
        self.nc = nc
        self.es = es
        self.ops = []
        self.streams = {e: [] for e in ENGS}
        self.chan_last = {}
        self.chan_count = {}
        self.bufs = []
        self.arena = es.enter_context(nc.sbuf_tensor("arena", [128, arena_f32], F32))
        self.cap = arena_f32 * 4
        self.top = 0
        self.base = 0
        self.psum = [self._ps("ps%d" % i) for i in range(8)]
        self.ps_i = 0
        self.ps_free = {}
        self.inst_map = {}

    def _ps(self, name):
        t = self.es.enter_context(self.nc.psum_tensor(name, [128, 512], F32))
        b = Buf(name, t, excl=True)
        self.bufs.append(b)
        return b

    def next_ps(self):
        for _ in range(8):
            b = self.psum[self.ps_i % 8]
            self.ps_i += 1
            if self.ps_free.get(b.name, True):
                self.ps_free[b.name] = False
                return b
        raise AssertionError("all PSUM banks live")

    def free_ps(self, b):
        self.ps_free[b.name] = True

    def alloc(self, name, free_shape, dt=F32):
        free_shape = tuple(int(x) for x in free_shape)
        n = int(np.prod(free_shape))
        nb = (n * DT_SIZE[dt] + 63) // 64 * 64
        off = self.top
        self.top += nb
        assert self.top <= self.cap, ("SBUF arena overflow", name, self.top, self.cap)
        ap = self.arena[:, off // 4:(off + nb) // 4]
        if dt != F32:
            ap = ap.bitcast(dt)
        ap = ap[:, :n]
        if len(free_shape) == 2:
            ap = ap.rearrange("p (a b) -> p a b", a=free_shape[0], b=free_shape[1])
        elif len(free_shape) == 3:
            ap = ap.rearrange("p (a b c) -> p a b c", a=free_shape[0], b=free_shape[1],
                              c=free_shape[2])
        b = Buf(name, ap)
        self.bufs.append(b)
        return b

    def ring(self, name, k, free_shape, dt=F32):
        return Ring([self.alloc("%s%d" % (name, i), free_shape, dt) for i in range(k)], name)

    def mark(self):
        self.base = self.top

    def release(self):
        self.top = self.base

    def _record(self, eng, fn, reads, writes, is_dma, chan, n_dma, unit=16):
        op = Op()
        op.unit = unit
        op.idx = len(self.ops)
        op.eng = eng
        op.fn = fn
        op.is_dma = is_dma
        op.chan = chan
        op.n_dma = n_dma
        op.val = None
        op.needs_inc = False
        op.barrier = False
        deps = {}

        def add(d, kind):
            if d is None:
                return
            if kind == "raw" or d not in deps:
                deps[d] = kind

        for b in reads:
            add(b.last_w, "raw")
            if b.excl:
                for a in b.last_acc:
                    add(a, "war")
        for b in writes:
            add(b.last_w, "waw")
            for r in b.readers:
                add(r, "war")
            if b.excl:
                for a in b.last_acc:
                    add(a, "war")
        if is_dma and chan in self.chan_last:
            add(self.chan_last[chan], "raw")
        for b in reads:
            if b not in writes:
                b.readers.append(op.idx)
            if b.excl:
                b.last_acc = [op.idx]
        for b in writes:
            b.last_w = op.idx
            b.readers = []
            if b.excl:
                b.last_acc = [op.idx]
        if is_dma:
            self.chan_last[chan] = op.idx
            self.chan_count[chan] = self.chan_count.get(chan, 0) + unit * n_dma
            op.val = self.chan_count[chan]
        op.deps = deps
        op.desc = "R[%s] W[%s]" % (",".join(b.name for b in reads), ",".join(b.name for b in writes))
        self.ops.append(op)
        self.streams[eng].append(op)
        return op

    def op(self, eng, fn, reads=(), writes=()):
        return self._record(eng, fn, list(reads), list(writes), False, None, 0)

    def dma(self, eng, fn, reads=(), writes=(), chan=None, n=1):
        assert chan is not None
        return self._record(eng, fn, list(reads), list(writes), True, chan, n)

    def cc(self, fn, chan):
        return self._record("pool", fn, [], [], True, chan, 1, unit=1)

    def barrier(self):
        bops = []
        for e in ENGS:
            op = Op()
            op.idx = len(self.ops)
            op.eng = e
            op.fn = None
            op.is_dma = False
            op.chan = None
            op.n_dma = 0
            op.val = None
            op.needs_inc = False
            op.barrier = True
            op.unit = 0
            op.deps = {}
            op.bvals = dict(self.chan_count)
            self.ops.append(op)
            self.streams[e].append(op)
            bops.append(op)
        for b in self.bufs:
            b.last_w = None
            b.readers = []
            b.last_acc = []
        self.chan_last = {}

    def emit(self):
        nc = self.nc
        ops = self.ops
        need = {}
        for op in ops:
            if op.barrier:
                continue
            w = []
            for d, kind in op.deps.items():
                p = ops[d]
                if p.is_dma:
                    w.append(d)
                    continue
                if p.eng == op.eng and not op.is_dma:
                    if kind != "raw" or op.eng == "pe":
                        continue
                w.append(d)
                p.needs_inc = True
            need[op.idx] = w
        for e in ENGS:
            last = None
            for op in self.streams[e]:
                if op.barrier:
                    if last is not None:
                        last.needs_inc = True
                elif not op.is_dma:
                    last = op
        ecount_at = {}
        for e in ENGS:
            c = 0
            for op in self.streams[e]:
                if op.barrier:
                    ecount_at[(e, op.idx)] = c
                elif not op.is_dma and op.needs_inc:
                    c += 1
                    op.val = c
        es = self.es
        esem = {e: es.enter_context(nc.semaphore("s_" + e)) for e in ENGS}
        csem = {c: es.enter_context(nc.semaphore("c_%s" % (c,))) for c in self.chan_count}
        self.n_sems = len(esem) + len(csem)
        block = es.enter_context(nc.Block())
        bar_groups = {}
        for op in ops:
            if op.barrier:
                g = op.idx - ENGS.index(op.eng)
                bar_groups.setdefault(g, {})[op.eng] = op

        plan = {}
        for ename in ENGS:
            waited = {}
            lst = []

            def wl(key, val, acc):
                if val > 0 and waited.get(key, 0) < val:
                    acc.append((key, val))
                    waited[key] = val

            for op in self.streams[ename]:
                acc = []
                if op.barrier:
                    g = op.idx - ENGS.index(ename)
                    for e2 in ENGS:
                        wl(("e", e2), ecount_at[(e2, bar_groups[g][e2].idx)], acc)
                    for c, v in op.bvals.items():
                        wl(("c", c), v, acc)
                    lst.append((acc, None))
                    continue
                for d in need[op.idx]:
                    p = ops[d]
                    wl(("c", p.chan) if p.is_dma else ("e", p.eng), p.val, acc)
                lst.append((acc, op))
            if ename == "sp":
                acc = []
                for c, tot in self.chan_count.items():
                    wl(("c", c), tot, acc)
                lst.append((acc, None))
            plan[ename] = lst
        semv = {}
        pos = {e: 0 for e in ENGS}
        progress = True
        while progress:
            progress = False
            for e in ENGS:
                while pos[e] < len(plan[e]):
                    acc, op = plan[e][pos[e]]
                    if any(semv.get(k, 0) < v for k, v in acc):
                        break
                    if op is not None:
                        if op.is_dma:
                            semv[("c", op.chan)] = semv.get(("c", op.chan), 0) + op.unit * op.n_dma
                        elif op.needs_inc:
                            semv[("e", e)] = semv.get(("e", e), 0) + 1
                    pos[e] += 1
                    progress = True
        for e in ENGS:
            if pos[e] < len(plan[e]):
                acc, op = plan[e][pos[e]]
                raise AssertionError("semaphore deadlock: engine %s stuck at %d/%d waiting %s (have %s)" % (
                    e, pos[e], len(plan[e]), acc, [(k, semv.get(k, 0)) for k, v in acc]))
        self.plan_sizes = {e: len(plan[e]) for e in ENGS}

        def semof(key):
            return csem[key[1]] if key[0] == "c" else esem[key[1]]

        def run_stream(ename):
            def body(eng):
                for acc, op in plan[ename]:
                    for key, val in acc:
                        eng.wait_ge(semof(key), val)
                    if op is None:
                        continue
                    r = op.fn(eng)
                    try:
                        rr_ = r[-1] if isinstance(r, (list, tuple)) else r
                        self.inst_map[str(rr_.ins.name)] = (ename, op.idx, op.desc)
                    except Exception:
                        pass
                    if op.is_dma:
                        rs = r if isinstance(r, (list, tuple)) else [r]
                        assert len(rs) == op.n_dma, (len(rs), op.n_dma)
                        for i in rs:
                            i.then_inc(csem[op.chan], op.unit)
                    elif op.needs_inc:
                        r.then_inc(esem[ename], 1)
            return body

        block.tensor(run_stream("pe"))
        block.scalar(run_stream("act"))
        block.vector(run_stream("dve"))
        block.gpsimd(run_stream("pool"))
        block.sync(run_stream("sp"))


class Ring:
    def __init__(self, bufs, name):
        self.bufs = bufs
        self.i = 0
        self.name = name

    def next(self):
        k = self.i % len(self.bufs)
        self.i += 1
        return self.bufs[k], "%s_%d" % (self.name, k)


def mm_group(S, out_buf, out_ap, pairs, extra_reads=()):
    n = len(pairs)

    def fn(e):
        r = None
        for i, (l, rr) in enumerate(pairs):
            r = e.matmul(out_ap, l, rr, start=(i == 0), stop=(i == n - 1))
        return r
    return fn


def chunked(v):
    v = np.asarray(v, np.float32)
    return np.ascontiguousarray(v.reshape(-1, 128).T)


class PP:
    def __init__(self):
        self.cols = {}
        self.n = 0

    def add(self, name, w):
        self.cols[name] = (self.n, w)
        self.n += w

    def sl(self, name, a=0, b=None):
        o, w = self.cols[name]
        if b is None:
            b = w
        return slice(o + a, o + b)


def pp_layout():
    P = PP()
    P.add("c", 8)
    for l in range(2):
        P.add("g_ffn1_%d" % l, 8)
        P.add("g_mix_%d" % l, 8)
        P.add("g_ffn2_%d" % l, 8)
        P.add("adab_%d" % l, 72)
        P.add("convw_%d" % l, 16)
        P.add("convb_%d" % l, 4)
        P.add("dtb_%d" % l, 4)
        P.add("alog_%d" % l, 4)
        P.add("dsk_%d" % l, 2)
        P.add("retgn_%d" % l, 2)
        P.add("ssmn_%d" % l, 4)
    P.add("g_fin", 8)
    P.add("finb", 16)
    P.add("ysel", 2)
    return P


PPL = pp_layout()


def pack_pp(inp, b, hh, lmap=(0, 1)):
    P = PPL
    a = np.zeros((128, P.n), np.float32)
    a[:, P.sl("c")] = chunked(inp["c"][b])
    for slot, L in enumerate(lmap):
        l = slot
        inp_l = {k: inp[k][L] for k in ("norm_ffn1", "norm_mix", "norm_ffn2", "ada_b", "conv_w", "conv_b", "dt_bias", "a_log", "d_skip", "ret_gn", "ssm_norm")}
        a[:, P.sl("g_ffn1_%d" % l)] = chunked(inp_l["norm_ffn1"])
        a[:, P.sl("g_mix_%d" % l)] = chunked(inp_l["norm_mix"])
        a[:, P.sl("g_ffn2_%d" % l)] = chunked(inp_l["norm_ffn2"])
        a[:, P.sl("adab_%d" % l)] = chunked(inp_l["ada_b"])
        ch = np.concatenate([np.arange(hh * 256, hh * 256 + 256),
                             512 + hh * 128 + np.arange(128),
                             768 + hh * 128 + np.arange(128)])
        cw = inp_l["conv_w"][:, ch]
        a[:, P.sl("convw_%d" % l)] = cw.reshape(4, 4, 128).transpose(2, 1, 0).reshape(128, 16)
        a[:, P.sl("convb_%d" % l)] = chunked(inp_l["conv_b"][ch])
        hs = np.arange(4 * hh, 4 * hh + 4)
        a[:, P.sl("dtb_%d" % l)] = np.broadcast_to(inp_l["dt_bias"][hs], (128, 4))
        a[:, P.sl("alog_%d" % l)] = np.broadcast_to(inp_l["a_log"][hs], (128, 4))
        dsk = inp_l["d_skip"][hs]
        a[:, P.sl("dsk_%d" % l)] = np.stack(
            [np.repeat(dsk[0:2], 64), np.repeat(dsk[2:4], 64)], axis=1)
        a[:, P.sl("retgn_%d" % l)] = chunked(inp_l["ret_gn"][hh * 256:hh * 256 + 256])
        a[:, P.sl("ssmn_%d" % l)] = chunked(inp_l["ssm_norm"])
    a[:, P.sl("g_fin")] = chunked(inp["final_norm"])
    a[:, P.sl("finb")] = chunked(inp["final_ada_b"])
    a[:, P.sl("ysel")] = np.array([1.0, 0.0] if hh == 0 else [0.0, 1.0], np.float32)[None, :]
    return a


class Ctx:
    pass


def setup_common(S, C, pp_d):
    C.pp = S.alloc("pp", (PPL.n,))
    S.dma("sp", lambda e: e.dma_start(out=C.pp[:], in_=pp_d), writes=[C.pp], chan="pp")
    C.cond = S.alloc("cond", (8,))
    S.op("act", lambda e: e.activation(out=C.cond[:], in_=C.pp[:, PPL.sl("c")], func=AF.Silu),
         reads=[C.pp], writes=[C.cond])
    C.ones_bf = S.alloc("ones_bf", (128,), BF16)
    S.op("pool", lambda e: e.memset(C.ones_bf[:], 1.0), writes=[C.ones_bf])
    C.epsc = S.alloc("epsc", (1,))
    S.op("pool", lambda e: e.memset(C.epsc[:], EPS), writes=[C.epsc])
    C.onec = S.alloc("onec", (1,))
    S.op("pool", lambda e: e.memset(C.onec[:], 1.0), writes=[C.onec])
    C.modv = S.alloc("modv", (24,))
    C.Asc = S.alloc("Asc", (8,))
    C.Gsc = S.alloc("Gsc", (8,))
    S.mark()


def compute_mod(S, C, adaw_d, ncols_total, j0, nj, adab_ap):
    S.release()
    mstage = S.ring("mstg", 3, (2048,))
    pm = S.next_ps()
    blocked = len(adaw_d.shape) == 3
    wv = None if blocked else adaw_d.rearrange("(kc p) f -> p kc f", p=128)
    for blk in range(nj * 4):
        col0 = j0 * 1024 + blk * 256
        st, ch = mstage.next()
        sv = st[:, 0:2048].rearrange("p (kc f) -> p kc f", kc=8, f=256)
        src = (adaw_d[col0 // 256].rearrange("p (kc f) -> p kc f", kc=8, f=256) if blocked
               else wv[:, :, col0:col0 + 256])
        S.dma("sp", lambda e, sv=sv, src=src: e.dma_start(out=sv, in_=src),
              writes=[st], chan=ch)
        for cc in range(2):
            oc = blk * 2 + cc

            def fn(e, sv=sv, cc=cc, oc=oc):
                r = None
                for kc in range(8):
                    r = e.matmul(pm[:, oc:oc + 1], sv[:, kc, cc * 128:(cc + 1) * 128],
                                 C.cond[:, kc:kc + 1], start=(kc == 0), stop=(kc == 7))
                return r
            S.op("pe", fn, reads=[st, C.cond], writes=[pm])
    n = nj * 8
    S.op("dve", lambda e: e.tensor_tensor(out=C.modv[:, 0:n], in0=pm[:, 0:n], in1=adab_ap,
                                          op=ALU.add), reads=[pm, C.pp], writes=[C.modv])
    S.free_ps(pm)
    S.barrier()


def mod_affine(S, C, gain_ap, gate_half):
    S.op("dve", lambda e: e.scalar_tensor_tensor(out=C.Asc[:], in0=C.modv[:, 8:16], scalar=1.0,
                                                 in1=gain_ap, op0=ALU.add, op1=ALU.mult),
         reads=[C.modv, C.pp], writes=[C.Asc])
    S.op("dve", lambda e: e.tensor_scalar(out=C.Gsc[:], in0=C.modv[:, 16:24], scalar1=1.0,
                                          scalar2=(0.5 if gate_half else 1.0),
                                          op0=ALU.add, op1=ALU.mult),
         reads=[C.modv], writes=[C.Gsc])


def load_h_tile(S, R, src_ap):
    ht, ch = R.htile.next()
    S.dma("sp", lambda e: e.dma_start(out=ht[:], in_=src_ap.rearrange("(c p) t -> p c t", p=128)),
          writes=[ht], chan=ch)
    return ht


def norm_mod_tile(S, C, R, src_ap, un_ap, un_buf, ht=None):
    if ht is None:
        ht = load_h_tile(S, R, src_ap)
    sq, _ = R.sq.next()
    S.op("act", lambda e: e.activation(out=sq[:], in_=ht[:], func=AF.Square), reads=[ht], writes=[sq])
    pss = S.next_ps()

    def fn(e):
        r = None
        for c in range(8):
            r = e.matmul(pss[:], C.ones_bf[:], sq[:, c, :], start=(c == 0), stop=(c == 7))
        return r
    S.op("pe", fn, reads=[sq, C.ones_bf], writes=[pss])
    lnv, _ = R.rs.next()
    S.op("act", lambda e: e.activation(out=lnv[:], in_=pss[:], func=AF.Ln, bias=C.epsc[:],
                                       scale=1.0 / D), reads=[pss, C.epsc], writes=[lnv])
    S.free_ps(pss)
    rstd, _ = R.rs.next()
    S.op("act", lambda e: e.activation(out=rstd[:], in_=lnv[:], func=AF.Exp, scale=-0.5),
         reads=[lnv], writes=[rstd])
    for c in range(8):
        tmp, _ = R.tmp.next()
        S.op("dve", lambda e, c=c, tmp=tmp: e.scalar_tensor_tensor(
            out=tmp[:], in0=ht[:, c, :], scalar=C.Asc[:, c:c + 1], in1=rstd[:],
            op0=ALU.mult, op1=ALU.mult), reads=[ht, C.Asc, rstd], writes=[tmp])
        S.op("act", lambda e, c=c, tmp=tmp: e.activation(
            out=un_ap[:, c, :], in_=tmp[:], func=AF.Identity, bias=C.modv[:, c:c + 1], scale=1.0),
            reads=[tmp, C.modv], writes=[un_buf])


def load_cast(S, C, dram_view, shape3, wring, stage, eng="act"):
    a, b = shape3
    st, ch = stage.next()
    sv = st[:, 0:a * b].rearrange("p (a b) -> p a b", a=a, b=b)
    S.dma("sp", lambda e: e.dma_start(out=sv, in_=dram_view), writes=[st], chan=ch)
    wb, _ = wring.next()
    CP(S, eng, (wb, wb[:]), (st, sv))
    return wb


def ffn_phase(S, C, hsrc, hdst, wg_d, wu_d, wd_d, ntok):
    TT = 1024
    NTT = TT // 512
    R = Ctx()
    S.release()
    un = S.alloc("un", (8, TT), BF16)
    hid = S.alloc("hid", (NFC, TT), BF16)
    R.htile = S.ring("ht", 1, (8, 512))
    R.sq = S.ring("sq", 1, (8, 512), BF16)
    R.rs = S.ring("rs", 4, (512,))
    R.tmp = S.ring("tmp", 3, (512,))
    wgr = S.ring("wgb", 2, (8, 256), BF16)
    wur = S.ring("wub", 2, (8, 256), BF16)
    wdr = S.ring("wdb", 2, (NFC, 128), BF16)
    sgr = S.ring("sg", 3, (512,), BF16)
    resr = S.ring("res", 3, (512,))
    outr = S.ring("outt", 3, (512,))
    stage = S.ring("stg", 3, (2816,))
    blocked = len(wg_d.shape) == 3
    if blocked:
        wg_blk = lambda fp: wg_d[fp].rearrange("p (a b) -> p a b", a=8, b=256)
        wu_blk = lambda fp: wu_d[fp].rearrange("p (a b) -> p a b", a=8, b=256)
        wd_blk = lambda dc: wd_d[dc].rearrange("p (a b) -> p a b", a=NFC, b=128)
    else:
        wgv = wg_d.rearrange("(kc p) f -> p kc f", p=128)
        wuv = wu_d.rearrange("(kc p) f -> p kc f", p=128)
        wdv = wd_d.rearrange("(fc p) d -> p fc d", p=128)
        wg_blk = lambda fp: wgv[:, :, fp * 256:(fp + 1) * 256]
        wu_blk = lambda fp: wuv[:, :, fp * 256:(fp + 1) * 256]
        wd_blk = lambda dc: wdv[:, :, dc * 128:(dc + 1) * 128]
    uns = [un, S.alloc("un_b", (8, TT), BF16)]
    nst = ntok // TT

    def p0(st_j):
        u = uns[st_j % 2]
        for tt in range(NTT):
            t0 = st_j * TT + tt * 512
            norm_mod_tile(S, C, R, hsrc[:, t0:t0 + 512], u[:, :, tt * 512:(tt + 1) * 512], u)

    p0(0)
    for st_i in range(nst):
        T0 = st_i * TT
        un = uns[st_i % 2]
        for fp in range(NFC // 2):
            wgb = load_cast(S, C, wg_blk(fp), (8, 256), wgr, stage, "act")
            wub = load_cast(S, C, wu_blk(fp), (8, 256), wur, stage, "dve")
            for tt in range(NTT):
                for j in range(2):
                    f = fp * 2 + j
                    pg = S.next_ps()
                    S.op("pe", mm_group(S, pg, pg[:], [
                        (wgb[:, kc, j * 128:(j + 1) * 128], un[:, kc, tt * 512:(tt + 1) * 512])
                        for kc in range(8)]), reads=[wgb, un], writes=[pg])
                    pu = S.next_ps()
                    S.op("pe", mm_group(S, pu, pu[:], [
                        (wub[:, kc, j * 128:(j + 1) * 128], un[:, kc, tt * 512:(tt + 1) * 512])
                        for kc in range(8)]), reads=[wub, un], writes=[pu])
                    sg, _ = sgr.next()
                    S.op("act", lambda e, sg=sg, pg=pg: e.activation(out=sg[:], in_=pg[:], func=AF.Silu),
                         reads=[pg], writes=[sg])
                    S.op("dve", lambda e, sg=sg, pu=pu, f=f, tt=tt: e.tensor_tensor(
                        out=hid[:, f, tt * 512:(tt + 1) * 512], in0=sg[:], in1=pu[:], op=ALU.mult),
                        reads=[sg, pu], writes=[hid])
                    S.free_ps(pg)
                    S.free_ps(pu)
        if st_i + 1 < nst:
            p0(st_i + 1)
        for dc in range(8):
            wdb = load_cast(S, C, wd_blk(dc), (NFC, 128), wdr, stage, "act" if dc % 2 == 0 else "dve")
            for tt in range(NTT):
                t0 = T0 + tt * 512
                res, ch = resr.next()
                S.dma("sp", lambda e, res=res, dc=dc, t0=t0: e.dma_start(
                    out=res[:], in_=hsrc[dc * 128:(dc + 1) * 128, t0:t0 + 512]), writes=[res], chan=ch)
                po = S.next_ps()
                S.op("pe", mm_group(S, po, po[:], [
                    (wdb[:, fc, :], hid[:, fc, tt * 512:(tt + 1) * 512]) for fc in range(NFC)]),
                    reads=[wdb, hid], writes=[po])
                ot, ch2 = outr.next()
                S.op("dve", lambda e, ot=ot, po=po, res=res, dc=dc: e.scalar_tensor_tensor(
                    out=ot[:], in0=po[:], scalar=C.Gsc[:, dc:dc + 1], in1=res[:],
                    op0=ALU.mult, op1=ALU.add), reads=[po, C.Gsc, res], writes=[ot])
                S.free_ps(po)
                S.dma("pool", lambda e, ot=ot, dc=dc, t0=t0: e.dma_start(
                    out=hdst[dc * 128:(dc + 1) * 128, t0:t0 + 512], in_=ot[:]), reads=[ot], chan=ch2)
    S.barrier()


def build_A(ntok=NTOK):
    nc = bass.Bass("TRN2", target_bir_lowering=False)
    hin = nc.dram_tensor("hin", [D, ntok], F32, kind="ExternalInput").ap()
    pp_d = nc.dram_tensor("pp", [128, PPL.n], F32, kind="ExternalInput").ap()
    adaw = nc.dram_tensor("adaw", [D, 9 * D], F32, kind="ExternalInput").ap()
    wg = nc.dram_tensor("wg", [D, DFF], F32, kind="ExternalInput").ap()
    wu = nc.dram_tensor("wu", [D, DFF], F32, kind="ExternalInput").ap()
    wd = nc.dram_tensor("wd", [DFF, D], F32, kind="ExternalInput").ap()
    hout = nc.dram_tensor("hout", [D, ntok], F32, kind="ExternalOutput").ap()
    with contextlib.ExitStack() as es:
        S = Sched(nc, es)
        C = Ctx()
        setup_common(S, C, pp_d)
        compute_mod(S, C, adaw, 9 * D, 0, 3, C.pp[:, PPL.sl("adab_0", 0, 24)])
        mod_affine(S, C, C.pp[:, PPL.sl("g_ffn1_0")], True)
        ffn_phase(S, C, hin, hout, wg, wu, wd, ntok)
        S.emit()
    return nc


def ACT(S, o, i, func, bias=None, scale=1.0, extra=()):
    ob, oap = o
    ib, iap = i
    rd = [ib] + list(extra)
    if bias is not None:
        rd.append(bias[0])
        S.op("act", lambda e: e.activation(out=oap, in_=iap, func=func, bias=bias[1], scale=scale),
             reads=rd, writes=[ob])
    else:
        S.op("act", lambda e: e.activation(out=oap, in_=iap, func=func, scale=scale),
             reads=rd, writes=[ob])


def TT(S, eng, o, a, b, op):
    S.op(eng, lambda e: e.tensor_tensor(out=o[1], in0=a[1], in1=b[1], op=op),
         reads=[a[0], b[0]], writes=[o[0]])


def TS(S, eng, o, a, s1, s2=None, op0=ALU.mult, op1=None, sb=()):
    if op1 is None:
        S.op(eng, lambda e: e.tensor_scalar(out=o[1], in0=a[1], scalar1=s1, scalar2=None, op0=op0),
             reads=[a[0]] + list(sb), writes=[o[0]])
    else:
        S.op(eng, lambda e: e.tensor_scalar(out=o[1], in0=a[1], scalar1=s1, scalar2=s2, op0=op0, op1=op1),
             reads=[a[0]] + list(sb), writes=[o[0]])


def STT(S, o, a, sc, b, op0, op1, sb=()):
    S.op("dve", lambda e: e.scalar_tensor_tensor(out=o[1], in0=a[1], scalar=sc, in1=b[1], op0=op0, op1=op1),
         reads=[a[0], b[0]] + list(sb), writes=[o[0]])


def CP(S, eng, o, a):
    if eng == "act":
        S.op("act", lambda e: e.activation(out=o[1], in_=a[1], func=AF.Copy), reads=[a[0]], writes=[o[0]])
    else:
        S.op(eng, lambda e: e.tensor_copy(out=o[1], in_=a[1]), reads=[a[0]], writes=[o[0]])


class CL:
    U = 0
    TRI = 128
    BLK = 256
    NEGM = 384
    ONES = 512
    DMAT = 640
    QDEC = 1664
    KDEC = 2688
    HMASK = 2692
    GL = 2694
    N = 2696
    IDENT = 0
    SBM = 128
    TRIB = 128 + 2048
    BLKB = 128 + 2048 + 128
    NB16 = 128 + 2048 + 256


def ret_gamma(hglob):
    return 1.0 - 2.0 ** (-5.0 - hglob)


def make_consts(hh):
    c = np.zeros((128, CL.N), np.float32)
    p = np.arange(128)
    j = p[:, None]
    s = p[None, :]
    same = (j // 64) == (s // 64)
    c[:, CL.U:CL.U + 128] = (j > s)
    c[:, CL.TRI:CL.TRI + 128] = same & (j <= s)
    c[:, CL.BLK:CL.BLK + 128] = same
    c[:, CL.NEGM:CL.NEGM + 128] = np.where(same & (s >= j), 0.0, -30000.0)
    c[:, CL.ONES:CL.ONES + 128] = 1.0
    sc = 128.0 ** -0.5
    for hl in range(2):
        lg = math.log1p(-(2.0 ** (-5.0 - (2 * hh + hl))))
        dm = np.where(same, sc * np.exp(lg * np.abs(j - s)), 0.0)
        c[:, CL.DMAT + hl * 512:CL.DMAT + (hl + 1) * 512] = np.tile(dm, (1, 4))
        l = np.arange(512)
        c[:, CL.QDEC + hl * 512:CL.QDEC + (hl + 1) * 512] = np.exp(lg * ((l % 64) + 1.0))[None, :]
        kd = sc * np.exp(lg * (63.0 - (p % 64)))
        c[:, CL.KDEC + hl * 2 + 0] = np.where(p < 64, kd, 0.0)
        c[:, CL.KDEC + hl * 2 + 1] = np.where(p >= 64, kd, 0.0)
    for hl in range(2):
        c[:, CL.GL + hl] = ret_gamma(2 * hh + hl) ** 64
    c[:, CL.HMASK + 0] = (p < 64)
    c[:, CL.HMASK + 1] = (p >= 64)
    import ml_dtypes
    b = np.zeros((128, CL.NB16), np.float32)
    b[:, CL.IDENT:CL.IDENT + 128] = np.eye(128)
    t = np.arange(512)[None, :]
    for i in range(4):
        b[:, CL.SBM + i * 512:CL.SBM + (i + 1) * 512] = ((i * 128 + j) < t)
    b[:, CL.TRIB:CL.TRIB + 128] = c[:, CL.TRI:CL.TRI + 128]
    b[:, CL.BLKB:CL.BLKB + 128] = c[:, CL.BLK:CL.BLK + 128]
    return c, b.astype(ml_dtypes.bfloat16)


def make_rope():
    half = 64
    inv = (10000.0 ** (-np.arange(half, dtype=np.float32) / half)).astype(np.float32)
    pos = np.arange(SEQ, dtype=np.float32)
    ang = (pos[:, None] * inv[None, :]).astype(np.float32)
    cos = np.cos(ang).astype(np.float32).T
    sin = np.sin(ang).astype(np.float32).T
    r = np.zeros((2, 128, SEQ), np.float32)
    r[0, :64] = cos
    r[0, 64:] = cos
    r[1, :64] = -sin
    r[1, 64:] = sin
    return r


WCOLS = 3080


def win_cols(hh):
    h0, h1 = 2 * hh, 2 * hh + 1
    a = np.arange(128)
    pm = (a + 64) % 128
    cols = []
    for base in (0, 512):
        cols += [base + h0 * 128 + a, base + h1 * 128 + a, base + h0 * 128 + pm, base + h1 * 128 + pm]
    cols = [cols[0], cols[1], cols[2], cols[3], cols[4], cols[5], cols[6], cols[7]]
    cols += [1536 + h0 * 128 + a, 1536 + h1 * 128 + a]
    cols += [2048 + h0 * 128 + a, 2048 + h1 * 128 + a]
    cols += [2560 + h0 * 128 + a, 2560 + h1 * 128 + a]
    cols += [3584 + hh * 256 + a, 3584 + hh * 256 + 128 + a]
    cols += [4096 + hh * 256 + a, 4096 + hh * 256 + 128 + a]
    cols += [4096 + 512 + hh * 128 + a, 4096 + 768 + hh * 128 + a]
    cols += [1024 + h0 * 128 + a, 1024 + h1 * 128 + a]
    cols += [3072 + h0 * 128 + a, 3072 + h1 * 128 + a]
    cols += [5120 + 4 * hh + np.arange(4), np.zeros(4, np.int64)]
    return np.concatenate(cols)


def mixer1_phase(S, C, l, hh, hsrc, win_d, cst_d, cstb_d, rope_d, qs_d, kt_d, v_d, y_d, yrow, ntiles=16, dbg=9):
    R = Ctx()
    S.release()
    cst = S.alloc("cst", (CL.N,))
    cstb = S.alloc("cstb", (CL.NB16,), BF16)
    S.dma("sp", lambda e: e.dma_start(out=cst[:], in_=cst_d), writes=[cst], chan="cst")
    S.dma("sp", lambda e: e.dma_start(out=cstb[:], in_=cstb_d), writes=[cstb], chan="cstb")
    ident = (cstb, cstb[:, CL.IDENT:CL.IDENT + 128])
    winb = S.alloc("winb", (8, 3200), BF16)
    S32 = [S.alloc("S32_%d" % h, (128,)) for h in range(2)]
    st32 = S.alloc("st32", (256,))
    cb = [S.alloc("cb%d" % i, (515,)) for i in range(4)]
    aneg = S.alloc("aneg", (4,))
    for h in range(2):
        S.op("pool", lambda e, h=h: e.memset(S32[h][:], 0.0), writes=[S32[h]])
    S.op("pool", lambda e: e.memset(st32[:], 0.0), writes=[st32])
    for i in range(4):
        S.op("pool", lambda e, i=i: e.memset(cb[i][:], 0.0), writes=[cb[i]])
    ACT(S, (aneg, aneg[:]), (C.pp, C.pp[:, PPL.sl("alog_%d" % l)]), AF.Exp)
    TS(S, "dve", (aneg, aneg[:]), (aneg, aneg[:]), -1.0)
    top1 = S.top
    stg = S.ring("wstg", 2, (8, 256))
    wv = win_d.rearrange("(kc p) f -> p kc f", p=128)
    for b0 in range(0, WCOLS, 256):
        w = min(256, WCOLS - b0)
        st, ch = stg.next()
        S.dma("sp", lambda e, st=st, b0=b0, w=w: e.dma_start(out=st[:, :, 0:w], in_=wv[:, :, b0:b0 + w]),
              writes=[st], chan=ch)
        CP(S, "act" if (b0 // 256) % 2 == 0 else "dve", (winb, winb[:, :, b0:b0 + w]), (st, st[:, :, 0:w]))
    S.barrier()
    S.top = top1
    R.htile = S.ring("ht", 1, (8, 512))
    R.sq = S.ring("sq", 1, (8, 512), BF16)
    R.rs = S.ring("rs", 4, (512,))
    R.tmp = S.ring("tmp", 4, (512,))
    un = S.alloc("un", (8, 512), BF16)
    roper = S.ring("rope", 2, (2, 512))
    qrot = S.ring("qrot", 2, (512,), BF16)
    qdec = S.ring("qdec", 2, (512,), BF16)
    krot = S.ring("krot", 2, (512,), BF16)
    sgr = S.ring("sgr", 2, (512,), BF16)
    b16 = S.ring("b16", 4, (512,), BF16)
    sz = [S.alloc("sz%d" % i, (512,), BF16) for i in range(2)]
    xc = [S.alloc("xc%d" % i, (512,), BF16) for i in range(4)]
    Vr = S.alloc("Vr", (4, 256), BF16)
    vs = S.ring("vs", 2, (256,), BF16)
    mret = S.alloc("mret", (512,), BF16)
    ktok = S.alloc("ktok", (2, 512), BF16)
    Sb = [S.alloc("Sb%d" % i, (128,), BF16) for i in range(8)]
    sm = S.alloc("sm", (8, 16))
    cdec = S.alloc("cdec", (8, 4))
    Rt = S.ring("Rt", 2, (2, 512), BF16)
    smb = S.alloc("smb", (2, 16), BF16)
    Et = S.ring("Et", 2, (4, 128))
    tmpd = S.ring("tmpd", 3, (4, 128))
    decT = S.ring("decT", 2, (4, 128), BF16)
    mT = [S.alloc("mT%d" % i, (4, 128), BF16) for i in range(4)]
    CTd = [S.alloc("CTd%d" % i, (4, 128), BF16) for i in range(4)]
    xdt = S.alloc("xdt", (4, 256), BF16)
    xdtd = S.alloc("xdtd", (2, 4, 256), BF16)
    dendm = S.alloc("dendm", (2, 16))
    Btok = S.alloc("Btok", (4, 128), BF16)
    stb = [S.alloc("stb%d" % i, (256,), BF16) for i in range(8)]
    tmpS = S.alloc("tmpS", (256,))
    yo = S.ring("yo", 3, (512,), BF16)
    trib = (cstb, cstb[:, CL.TRIB:CL.TRIB + 128])
    blkb = (cstb, cstb[:, CL.BLKB:CL.BLKB + 128])
    convw = C.pp[:, PPL.sl("convw_%d" % l)]
    convb = C.pp[:, PPL.sl("convb_%d" % l)]
    dsk = C.pp[:, PPL.sl("dsk_%d" % l)]
    retgn = C.pp[:, PPL.sl("retgn_%d" % l)]
    dtb = C.pp[:, PPL.sl("dtb_%d" % l)]
    sm3 = lambda i: sm[:, i, :].rearrange("p (b h) -> p b h", b=4, h=4)
    bc4 = lambda ap: ap.unsqueeze(1).to_broadcast([128, 4, 4])

    import os
    NKC = int(os.environ.get("PROJ_KC", "8"))
    PMODE = os.environ.get("PROJ_MODE", "")

    def proj_fm(ci):
        p = S.next_ps()
        if PMODE == "skip":
            return p
        if PMODE == "swap":
            S.op("pe", mm_group(S, p, p[:], [(un[:, kc, 0:128], winb[:, kc, ci * 128:ci * 128 + 512])
                                            for kc in range(NKC)]), reads=[winb, un], writes=[p])
            return p
        S.op("pe", mm_group(S, p, p[:], [(winb[:, kc, ci * 128:(ci + 1) * 128], un[:, kc, :])
                                        for kc in range(NKC)]), reads=[winb, un], writes=[p])
        return p

    for g in range(ntiles):
        t0 = g * 512
        hsrc_of = lambda gg: (hsrc(gg) if callable(hsrc) else hsrc[:, gg * 512:(gg + 1) * 512])
        if g == 0:
            ht_next = load_h_tile(S, R, hsrc_of(0))
        norm_mod_tile(S, C, R, None, un[:], un, ht=ht_next)
        if g + 1 < ntiles:
            ht_next = load_h_tile(S, R, hsrc_of(g + 1))
        ydg, yc0 = (y_d(g) if callable(y_d) else (y_d, t0))

        def load_rope(gg):
            rt, chr_ = roper.next()
            S.dma("sp", lambda e, rt=rt, gg=gg: e.dma_start(
                out=rt[:], in_=rope_d[:, :, gg * 512:(gg + 1) * 512].rearrange("a p t -> p a t")),
                writes=[rt], chan=chr_)
            return rt
        if g == 0:
            rope_next = load_rope(0)
        ropeT = rope_next
        if g + 1 < ntiles:
            rope_next = load_rope(g + 1)
        cosT = (ropeT, ropeT[:, 0, :])
        sinT = (ropeT, ropeT[:, 1, :])

        def rope(ci, cip, outb):
            p1 = proj_fm(ci)
            p2 = proj_fm(cip)
            t1, _ = R.tmp.next()
            t2, _ = R.tmp.next()
            if dbg <= 2.0001:
                S.free_ps(p1)
                S.free_ps(p2)
                return
            if dbg <= 2.0002:
                CP(S, "dve", (t1, t1[:]), (p1, p1[:]))
                CP(S, "dve", (t2, t2[:]), (p2, p2[:]))
                S.free_ps(p1)
                S.free_ps(p2)
                return
            TT(S, "dve", (t1, t1[:]), (p1, p1[:]), cosT, ALU.mult)
            TT(S, "dve", (t2, t2[:]), (p2, p2[:]), sinT, ALU.mult)
            S.free_ps(p1)
            S.free_ps(p2)
            if dbg <= 2.001:
                return
            TT(S, "pool", (outb, outb[:]), (t1, t1[:]), (t2, t2[:]), ALU.add)

        pdt = S.next_ps()
        for blk in range(4):
            ptm = S.next_ps()
            S.op("pe", mm_group(S, ptm, ptm[:], [(un[:, kc, blk * 128:(blk + 1) * 128], winb[:, kc, 2560:3072])
                                                for kc in range(8)]), reads=[winb, un], writes=[ptm])
            CP(S, "act", (Vr, Vr[:, blk, :]), (ptm, ptm[:, 0:256]))
            v1, chv = vs.next()
            CP(S, "dve", (v1, v1[:]), (ptm, ptm[:, 256:512]))
            S.free_ps(ptm)
            S.dma("sp", lambda e, v1=v1, blk=blk, t0=t0: e.dma_start(
                out=v_d[t0 + blk * 128:t0 + (blk + 1) * 128, :], in_=v1[:]), reads=[v1], chan=chv)
            S.op("pe", mm_group(S, pdt, pdt[:, blk * 128:(blk + 1) * 128],
                                [(un[:, kc, blk * 128:(blk + 1) * 128], winb[:, kc, 2948:3076]) for kc in range(8)]),
                 reads=[winb, un], writes=[pdt])
        TT(S, "dve", (sm, sm3(0)), (pdt, pdt[:].rearrange("p (b c) -> p b c", b=4, c=128)[:, :, 124:128]),
           (C.pp, bc4(dtb)), ALU.add)
        S.free_ps(pdt)
        ACT(S, (sm, sm[:, 1, :]), (sm, sm[:, 0, :]), AF.Exp)
        ACT(S, (sm, sm[:, 2, :]), (sm, sm[:, 1, :]), AF.Ln, bias=(C.onec, C.onec[:]))
        TT(S, "dve", (sm, sm3(3)), (sm, sm3(2)), (aneg, bc4(aneg[:])), ALU.mult)
        CP(S, "dve", (smb, smb[:, 0, :]), (sm, sm[:, 3, :]))
        TT(S, "dve", (smb, smb[:, 1, :]), (sm, sm[:, 3, :]), (smb, smb[:, 0, :]), ALU.subtract)
        if dbg <= 1:
            continue
        for h in range(2):
            qr, _ = qrot.next()
            rope(0 + h, 2 + h, qr)
            if dbg <= 2.002:
                continue
            qd, _ = qdec.next()
            TT(S, "pool", (qd, qd[:]), (qr, qr[:]), (cst, cst[:, CL.QDEC + h * 512:CL.QDEC + (h + 1) * 512]), ALU.mult)
            if dbg <= 2.01:
                continue
            kr, _ = krot.next()
            rope(4 + h, 6 + h, kr)
            if dbg <= 2.02:
                continue
            pg = proj_fm(8 + h)
            sg, _ = sgr.next()
            ACT(S, (sg, sg[:]), (pg, pg[:]), AF.Silu)
            S.free_ps(pg)
            if dbg <= 2.03:
                continue
            pq = proj_fm(10 + h)
            qsb, chq = b16.next()
            ACT(S, (qsb, qsb[:]), (pq, pq[:]), AF.Copy, scale=128.0 ** -0.5)
            S.free_ps(pq)
            S.dma("sp", lambda e, qsb=qsb, h=h, t0=t0: e.dma_start(
                out=qs_d[h * 128:(h + 1) * 128, t0:t0 + 512], in_=qsb[:]), reads=[qsb], chan=chq)
            if dbg <= 2.04:
                continue
            pk = proj_fm(12 + h)
            ksb, chk = b16.next()
            CP(S, "dve", (ksb, ksb[:]), (pk, pk[:]))
            S.free_ps(pk)
            S.dma("sp", lambda e, ksb=ksb, h=h, t0=t0: e.dma_start(
                out=kt_d[h * 128:(h + 1) * 128, t0:t0 + 512], in_=ksb[:]), reads=[ksb], chan=chk)
            if dbg <= 2.1:
                continue
            psc = S.next_ps()

            def fn_sc(e, kr=kr, qr=qr, psc=psc):
                r = None
                for blk in range(4):
                    sl = slice(blk * 128, (blk + 1) * 128)
                    r = e.matmul(psc[:, sl], kr[:, sl], qr[:, sl], start=True, stop=True)
                return r
            S.op("pe", fn_sc, reads=[kr, qr], writes=[psc])
            TT(S, "dve", (mret, mret[:]), (psc, psc[:]), (cst, cst[:, CL.DMAT + h * 512:CL.DMAT + (h + 1) * 512]), ALU.mult)
            S.free_ps(psc)
            if dbg <= 2.2:
                continue
            ptk = S.next_ps()
            ptkb = ptk[:].bitcast(BF16)

            def fn_tk(e, kr=kr, ptkb=ptkb):
                r = None
                for blk in range(4):
                    sl = slice(blk * 128, (blk + 1) * 128)
                    r = e.transpose(ptkb[:, sl], kr[:, sl], ident[1])
                return r
            S.op("pe", fn_tk, reads=[kr, cstb], writes=[ptk])
            for half in range(2):
                kc_ = CL.KDEC + h * 2 + half
                TS(S, "dve", (ktok, ktok[:, half, :]), (ptk, ptkb[:, 0:512]), cst[:, kc_:kc_ + 1], sb=[cst])
            S.free_ps(ptk)
            if dbg <= 2.3:
                continue
            pkv = [S.next_ps(), S.next_ps()]

            def fn_kv(e, h=h, pkv=pkv):
                r = None
                for c in range(8):
                    blk, half = c // 2, c % 2
                    r = e.matmul(pkv[c // 4][:, (c % 4) * 128:(c % 4 + 1) * 128],
                                 ktok[:, half, blk * 128:(blk + 1) * 128], Vr[:, blk, h * 128:(h + 1) * 128],
                                 start=True, stop=True)
                return r
            S.op("pe", fn_kv, reads=[ktok, Vr], writes=pkv)
            for c in range(8):
                CP(S, "act", (Sb[c], Sb[c][:]), (S32[h], S32[h][:]))
                pk_ = pkv[c // 4]
                STT(S, (S32[h], S32[h][:]), (S32[h], S32[h][:]), cst[:, CL.GL + h:CL.GL + h + 1],
                    (pk_, pk_[:, (c % 4) * 128:(c % 4 + 1) * 128]), ALU.mult, ALU.add, sb=[cst])
            S.free_ps(pkv[0])
            S.free_ps(pkv[1])
            if dbg <= 2.4:
                continue
            py = S.next_ps()

            def fn_y(e, h=h, py=py, qd=qd):
                r = None
                for blk in range(4):
                    sl = slice(blk * 128, (blk + 1) * 128)
                    e.matmul(py[:, sl], Vr[:, blk, h * 128:(h + 1) * 128], mret[:, sl], start=True, stop=False)
                    for half in range(2):
                        s2 = slice(blk * 128 + half * 64, blk * 128 + (half + 1) * 64)
                        r = e.matmul(py[:, s2], Sb[blk * 2 + half][:], qd[:, s2], start=False, stop=(half == 1))
                return r
            S.op("pe", fn_y, reads=[Vr, mret, qd] + Sb, writes=[py])
            ysq, _ = b16.next()
            ACT(S, (ysq, ysq[:]), (py, py[:]), AF.Square)
            pss = S.next_ps()
            S.op("pe", lambda e, pss=pss, ysq=ysq: e.matmul(pss[:], C.ones_bf[:], ysq[:], start=True, stop=True),
                 reads=[C.ones_bf, ysq], writes=[pss])
            lnv, _ = R.rs.next()
            ACT(S, (lnv, lnv[:]), (pss, pss[:]), AF.Ln, bias=(C.epsc, C.epsc[:]), scale=1.0 / 128)
            S.free_ps(pss)
            rstd, _ = R.rs.next()
            ACT(S, (rstd, rstd[:]), (lnv, lnv[:]), AF.Exp, scale=-0.5)
            t1, _ = R.tmp.next()
            STT(S, (t1, t1[:]), (py, py[:]), retgn[:, h:h + 1], (rstd, rstd[:]), ALU.mult, ALU.mult, sb=[C.pp])
            S.free_ps(py)
            yb, chy = yo.next()
            TT(S, "pool", (yb, yb[:]), (t1, t1[:]), (sg, sg[:]), ALU.mult)
            r0 = yrow["ret%d" % h]
            S.dma("sp", lambda e, yb=yb, r0=r0, ydg=ydg, yc0=yc0: e.dma_start(out=ydg[r0:r0 + 128, yc0:yc0 + 512], in_=yb[:]),
                  reads=[yb], chan=chy)
        if dbg < 3:
            continue
        for c in range(2):
            pz = proj_fm(14 + c)
            ACT(S, (sz[c], sz[c][:]), (pz, pz[:]), AF.Silu)
            S.free_ps(pz)
        for ch in range(4):
            pc = proj_fm(16 + ch)
            S.op("pool", lambda e, ch=ch: e.tensor_copy(out=cb[ch][:, 0:3], in_=cb[ch][:, 512:515]),
                 reads=[cb[ch]], writes=[cb[ch]])
            CP(S, "act", (cb[ch], cb[ch][:, 3:515]), (pc, pc[:]))
            S.free_ps(pc)
            acc, _ = R.tmp.next()
            TS(S, "dve", (acc, acc[:]), (cb[ch], cb[ch][:, 0:512]), convw[:, ch * 4:ch * 4 + 1], sb=[C.pp])
            for j in range(1, 4):
                STT(S, (acc, acc[:]), (cb[ch], cb[ch][:, j:j + 512]), convw[:, ch * 4 + j:ch * 4 + j + 1],
                    (acc, acc[:]), ALU.mult, ALU.add, sb=[C.pp])
            ACT(S, (xc[ch], xc[ch][:]), (acc, acc[:]), AF.Silu, bias=(C.pp, convb[:, ch:ch + 1]))
        BT, CT = xc[2], xc[3]
        pcb = S.next_ps()

        def fn_cb(e, pcb=pcb):
            r = None
            for blk in range(4):
                sl = slice(blk * 128, (blk + 1) * 128)
                r = e.matmul(pcb[:, sl], BT[:, sl], CT[:, sl], start=True, stop=True)
            return r
        S.op("pe", fn_cb, reads=[BT, CT], writes=[pcb])
        ptx = S.next_ps()
        ptxb = ptx[:].bitcast(BF16)

        def fn_tx(e, ptxb=ptxb):
            r = None
            for blk in range(4):
                for c in range(2):
                    r = e.transpose(ptxb[:, blk * 256 + c * 128:blk * 256 + (c + 1) * 128],
                                    xc[c][:, blk * 128:(blk + 1) * 128], ident[1])
            return r
        S.op("pe", fn_tx, reads=[xc[0], xc[1], cstb], writes=[ptx])
        dt16 = sm[:, 2, :].unsqueeze(2).to_broadcast([128, 16, 64])
        v16 = lambda ap: ap.rearrange("p (a b) -> p a b", a=16, b=64)
        TT(S, "dve", (xdt, v16(xdt[:].rearrange("p a b -> p (a b)"))), (ptx, v16(ptxb[:, 0:1024])), (sm, dt16), ALU.mult)
        S.free_ps(ptx)
        ptB = S.next_ps()
        ptBb = ptB[:].bitcast(BF16)

        def fn_tb(e, ptBb=ptBb):
            r = None
            for blk in range(4):
                sl = slice(blk * 128, (blk + 1) * 128)
                r = e.transpose(ptBb[:, sl], BT[:, sl], ident[1])
            return r
        S.op("pe", fn_tb, reads=[BT, cstb], writes=[ptB])
        CP(S, "act", (Btok, Btok[:].rearrange("p a b -> p (a b)")), (ptB, ptBb[:, 0:512]))
        S.free_ps(ptB)
        for blk in range(4):
            Rb, _ = Rt.next()
            for hl in range(2):
                TT(S, "dve", (Rb, Rb[:, hl, :].rearrange("p (a b) -> p a b", a=4, b=128)),
                   (cstb, trib[1].unsqueeze(1).to_broadcast([128, 4, 128])),
                   (smb, smb[:, hl, blk * 4:(blk + 1) * 4].unsqueeze(2).to_broadcast([128, 4, 128])), ALU.mult)
            pab = S.next_ps()

            def fn_ab(e, pab=pab, Rb=Rb):
                e.matmul(pab[:], C.ones_bf[:], Rb[:, 0, :], start=True, stop=False)
                return e.matmul(pab[:], C.ones_bf[:], Rb[:, 1, :], start=False, stop=True)
            S.op("pe", fn_ab, reads=[C.ones_bf, Rb], writes=[pab])
            Eb, _ = Et.next()
            ACT(S, (Eb, Eb[:].rearrange("p a b -> p (a b)")), (pab, pab[:]), AF.Exp)
            for half in range(2):
                cc = blk * 2 + half
                CP(S, "pool", (cdec, cdec[:, cc, :]), (Eb, Eb[:, :, 63 + 64 * half]))
            tq, _ = tmpd.next()
            pab3 = pab[:].rearrange("p (a b) -> p a b", a=4, b=128)
            TT(S, "dve", (tq, tq[:]), (pab, pab3), (cstb, ident[1].unsqueeze(1).to_broadcast([128, 4, 128])), ALU.mult)
            S.op("dve", lambda e, tq=tq, blk=blk: e.tensor_reduce(
                out=sm[:, 4, blk * 4:(blk + 1) * 4], in_=tq[:], axis=mybir.AxisListType.X, op=ALU.add),
                reads=[tq], writes=[sm])
            TS(S, "dve", (sm, sm[:, 5, blk * 4:(blk + 1) * 4]), (sm, sm[:, 4, blk * 4:(blk + 1) * 4]), -1.0)
            for half in range(2):
                rows = slice(half * 64, (half + 1) * 64)
                CP(S, "act", (sm, sm[rows, 6, blk * 4:(blk + 1) * 4]), (pab, pab3[rows, :, 63 + 64 * half]))
            td, _ = tmpd.next()
            TT(S, "dve", (td, td[:]), (pab, pab[:].rearrange("p (a b) -> p a b", a=4, b=128)),
               (cst, cst[:, CL.NEGM:CL.NEGM + 128].unsqueeze(1).to_broadcast([128, 4, 128])), ALU.add)
            S.free_ps(pab)
            dT, _ = decT.next()
            for h4 in range(4):
                ACT(S, (dT, dT[:, h4, :]), (td, td[:, h4, :]), AF.Exp, bias=(sm, sm[:, 5, blk * 4 + h4:blk * 4 + h4 + 1]))
            TT(S, "dve", (mT[blk], mT[blk][:]), (dT, dT[:]),
               (pcb, pcb[:, blk * 128:(blk + 1) * 128].unsqueeze(1).to_broadcast([128, 4, 128])), ALU.mult)
            TT(S, "pool", (CTd[blk], CTd[blk][:]), (Eb, Eb[:]),
               (CT, CT[:, blk * 128:(blk + 1) * 128].unsqueeze(1).to_broadcast([128, 4, 128])), ALU.mult)
        S.free_ps(pcb)
        TT(S, "dve", (sm, sm[:, 6, :]), (sm, sm[:, 6, :]), (sm, sm[:, 4, :]), ALU.subtract)
        ACT(S, (sm, sm[:, 7, :]), (sm, sm[:, 6, :]), AF.Exp)
        for half in range(2):
            TS(S, "dve", (dendm, dendm[:, half, :]), (sm, sm[:, 7, :]), cst[:, CL.HMASK + half:CL.HMASK + half + 1], sb=[cst])
            TT(S, "pool", (xdtd, v16(xdtd[:, half, :, :].rearrange("p a b -> p (a b)"))),
               (xdt, v16(xdt[:].rearrange("p a b -> p (a b)"))),
               (dendm, dendm[:, half, :].unsqueeze(2).to_broadcast([128, 16, 64])), ALU.mult)
        for hf in range(2):
            pst = [S.next_ps(), S.next_ps()]

            def fn_st(e, hf=hf, pst=pst):
                r = None
                for k in range(4):
                    c = hf * 4 + k
                    blk, half = c // 2, c % 2
                    r = e.matmul(pst[k // 2][:, (k % 2) * 256:(k % 2 + 1) * 256], Btok[:, blk, :],
                                 xdtd[:, half, blk, :], start=True, stop=True)
                return r
            S.op("pe", fn_st, reads=[Btok, xdtd], writes=pst)
            for k in range(4):
                c = hf * 4 + k
                CP(S, "act", (stb[c], stb[c][:]), (st32, st32[:]))
                TT(S, "dve", (tmpS, tmpS[:].rearrange("p (a b) -> p a b", a=4, b=64)),
                   (st32, st32[:].rearrange("p (a b) -> p a b", a=4, b=64)),
                   (cdec, cdec[:, c, :].unsqueeze(2).to_broadcast([128, 4, 64])), ALU.mult)
                TT(S, "dve", (st32, st32[:]), (tmpS, tmpS[:]),
                   (pst[k // 2], pst[k // 2][:, (k % 2) * 256:(k % 2 + 1) * 256]), ALU.add)
            S.free_ps(pst[0])
            S.free_ps(pst[1])
        for c in range(2):
            yb, chy = yo.next()
            for hp in range(2):
                h4 = c * 2 + hp
                pyh = S.next_ps()

                def fn_yh(e, h4=h4, c=c, pyh=pyh):
                    r = None
                    for blk in range(4):
                        sl = slice(blk * 128, (blk + 1) * 128)
                        e.matmul(pyh[:, sl], xdt[:, blk, c * 128:(c + 1) * 128], mT[blk][:, h4, :], start=True, stop=False)
                        for half in range(2):
                            s2 = slice(blk * 128 + half * 64, blk * 128 + (half + 1) * 64)
                            r = e.matmul(pyh[:, s2], stb[blk * 2 + half][:, c * 128:(c + 1) * 128],
                                         CTd[blk][:, h4, half * 64:(half + 1) * 64], start=False, stop=(half == 1))
                    return r
                S.op("pe", fn_yh, reads=[xdt] + mT + stb + CTd, writes=[pyh])
                rows = slice(hp * 64, (hp + 1) * 64)
                t1, _ = R.tmp.next()
                STT(S, (t1, t1[rows, :]), (xc[c], xc[c][rows, :]), dsk[rows, c:c + 1], (pyh, pyh[rows, :]),
                    ALU.mult, ALU.add, sb=[C.pp])
                S.free_ps(pyh)
                TT(S, "pool", (yb, yb[rows, :]), (t1, t1[rows, :]), (sz[c], sz[c][rows, :]), ALU.mult)
            r0 = yrow["ssm%d" % c]
            S.dma("sp", lambda e, yb=yb, r0=r0, ydg=ydg, yc0=yc0: e.dma_start(out=ydg[r0:r0 + 128, yc0:yc0 + 512], in_=yb[:]),
                  reads=[yb], chan=chy)
    S.barrier()


def mixer2_phase(S, C, qs_d, kt_d, v_d, cst_d, cstb_d, y_d, yrow, ntiles=16):
    S.release()
    cst = S.alloc("cst2", (CL.N,))
    cstb = S.alloc("cstb2", (CL.NB16,), BF16)
    S.dma("sp", lambda e: e.dma_start(out=cst[:], in_=cst_d), writes=[cst], chan="cst")
    S.dma("sp", lambda e: e.dma_start(out=cstb[:], in_=cstb_d), writes=[cstb], chan="cstb")
    nk = ntiles * 512
    KT = S.alloc("KT", (2, nk), BF16)
    Vt = S.alloc("Vt", (ntiles * 4, 256), BF16)
    S.dma("sp", lambda e: e.dma_start(out=KT[:], in_=kt_d[:, 0:nk].rearrange("(h p) t -> p h t", p=128)),
          writes=[KT], chan="ktload")
    S.dma("sp", lambda e: e.dma_start(out=Vt[:], in_=v_d[0:nk, :].rearrange("(b p) c -> p b c", p=128)),
          writes=[Vt], chan="vload")
    Ur = S.alloc("Ur", (128,), BF16)
    onesr = C.ones_bf
    CP(S, "act", (Ur, Ur[:]), (cst, cst[:, CL.U:CL.U + 128]))
    qsr = S.ring("qs", 2, (2, 512), BF16)
    er = S.ring("e", 2, (512,))
    spr = S.ring("sp", 6, (512,))
    lkr = S.ring("lk", 4, (512,), BF16)
    Lsr = S.ring("Lsum", 4, (512,), BF16)
    argr = S.ring("arg", 4, (512,))
    wr = S.ring("w", 4, (512,), BF16)
    yo = S.ring("yo2", 2, (512,), BF16)
    f32v = lambda b: b[:]
    DEPTH = 3
    for g in range(ntiles):
        t0 = g * 512
        qs, chq = qsr.next()
        S.dma("sp", lambda e, qs=qs, t0=t0: e.dma_start(
            out=qs[:], in_=qs_d[:, t0:t0 + 512].rearrange("(h p) t -> p h t", p=128)), writes=[qs], chan=chq)
        for h in range(2):
            pyT = S.next_ps()
            nkb = 4 * g + 4
            kbs = list(range(nkb - 1, -1, -1))
            st = {}
            st2 = {}
            lsc = {"cur": None}

            def s0(idx, h=h, qs=qs, kbs=kbs, st=st):
                kb = kbs[idx]
                pz = S.next_ps()
                S.op("pe", lambda e, pz=pz, kb=kb: e.matmul(
                    pz[:], KT[:, h, kb * 128:(kb + 1) * 128], qs[:, h, :], start=True, stop=True),
                    reads=[KT, qs], writes=[pz])
                st[idx] = {"pz": pz}

            def s1(idx, st=st):
                d = st[idx]
                ee, _ = er.next()
                ACT(S, (ee, ee[:]), (d["pz"], d["pz"][:]), AF.Exp, scale=-1.0)
                sp, _ = spr.next()
                ACT(S, (sp, sp[:]), (ee, ee[:]), AF.Ln, bias=(C.onec, C.onec[:]))
                d["sp"] = sp

            def s2(idx, g=g, kbs=kbs, st=st):
                d = st[idx]
                ing = kbs[idx] - 4 * g
                lk, _ = lkr.next()
                STT(S, (lk, lk[:]), (d["sp"], d["sp"][:]), -1.0, (d["pz"], d["pz"][:]), ALU.mult, ALU.subtract)
                S.free_ps(d["pz"])
                d["msk"] = None
                if ing >= 0:
                    d["msk"] = (cstb, cstb[:, CL.SBM + ing * 512:CL.SBM + (ing + 1) * 512])
                    TT(S, "pool", (lk, lk[:]), (lk, lk[:]), d["msk"], ALU.mult)
                d["lk"] = lk

            def s3(idx, kbs=kbs, st=st, lsc=lsc):
                d = st[idx]
                first = idx == 0
                last = kbs[idx] == 0
                lk = d["lk"]
                pt = S.next_ps()
                Ls = lsc["cur"]

                def fn_t(e, pt=pt, lk=lk, first=first, Ls=Ls):
                    r = e.matmul(pt[:], Ur[:], lk[:], start=True, stop=first)
                    if not first:
                        r = e.matmul(pt[:], onesr[:], Ls[:], start=False, stop=True)
                    return r
                S.op("pe", fn_t, reads=[Ur, onesr, lk] + ([] if first else [Ls]), writes=[pt])
                if not last:
                    Ln_, _ = Lsr.next()
                    if first:
                        CP(S, "pool", (Ln_, Ln_[:]), (lk, lk[:]))
                    else:
                        TT(S, "pool", (Ln_, Ln_[:]), (Ls, Ls[:]), (lk, lk[:]), ALU.add)
                    lsc["cur"] = Ln_
                d["pt"] = pt

            def s4(idx, st=st):
                d = st[idx]
                ag, _ = argr.next()
                TT(S, "dve", (ag, ag[:]), (d["pt"], d["pt"][:]), (d["sp"], d["sp"][:]), ALU.subtract)
                S.free_ps(d["pt"])
                d["ag"] = ag

            def s5(idx, st=st):
                d = st[idx]
                w, _ = wr.next()
                ACT(S, (w, w[:]), (d["ag"], d["ag"][:]), AF.Exp)
                if d["msk"] is not None:
                    TT(S, "pool", (w, w[:]), (w, w[:]), d["msk"], ALU.mult)
                d["w"] = w

            def s6(idx, h=h, kbs=kbs, st=st, pyT=pyT):
                d = st.pop(idx)
                kb = kbs[idx]
                first = idx == 0
                last = kb == 0
                w = d["w"]
                S.op("pe", lambda e, w=w, kb=kb, first=first, last=last: e.matmul(
                    pyT[:], Vt[:, kb, h * 128:(h + 1) * 128], w[:], start=first, stop=last),
                    reads=[Vt, w], writes=[pyT])

            n = len(kbs)
            stages = [s0, s1, s2, s3, s4, s5, s6]
            for step in range(n + len(stages) - 1):
                for si, fn_s in enumerate(stages):
                    i = step - si
                    if 0 <= i < n:
                        fn_s(i)
            yb, chy = yo.next()
            CP(S, "act", (yb, yb[:]), (pyT, pyT[:]))
            S.free_ps(pyT)
            r0 = yrow["sb%d" % h]
            ydg, yc0 = (y_d(g) if callable(y_d) else (y_d, t0))
            S.dma("sp", lambda e, yb=yb, r0=r0, ydg=ydg, yc0=yc0: e.dma_start(out=ydg[r0:r0 + 128, yc0:yc0 + 512], in_=yb[:]),
                  reads=[yb], chan=chy)
    S.barrier()


def wout_phase(S, C, l, y_d, ycol0, wout_d, hsrc, hdst, ntok, y2_d=None, ssm_pos=(8, 9, 10, 11)):
    S.release()
    wob = S.alloc("wob", (12, D), BF16)
    stg = S.ring("wostg", 2, (12, 256))
    wv = wout_d.rearrange("(fc p) d -> p fc d", p=128)
    for b0 in range(0, D, 256):
        st, ch = stg.next()
        S.dma("sp", lambda e, st=st, b0=b0: e.dma_start(out=st[:], in_=wv[:, :, b0:b0 + 256]), writes=[st], chan=ch)
        CP(S, "act" if (b0 // 256) % 2 == 0 else "dve", (wob, wob[:, :, b0:b0 + 256]), (st, st[:]))
    ytr = S.ring("yt", 2, (12, 512), BF16)
    yt2r = S.ring("yt2", 2, (12, 512), BF16)
    sqr = S.ring("ysq", 1, (4, 512), BF16)
    rsr = S.ring("rs", 2, (512,))
    tmpr = S.ring("tmp", 2, (512,))
    resr = S.ring("res", 3, (512,))
    outr = S.ring("outt", 3, (512,))
    ssmn = C.pp[:, PPL.sl("ssmn_%d" % l)]
    for tt in range(ntok // 512):
        t0 = tt * 512
        yt, chy = ytr.next()
        S.dma("sp", lambda e, yt=yt, t0=t0, tt=tt: e.dma_start(
            out=yt[:], in_=(y_d(tt) if callable(y_d) else y_d[:, ycol0 + t0:ycol0 + t0 + 512]).rearrange("(c p) t -> p c t", p=128)),
            writes=[yt], chan=chy)
        if y2_d is not None:
            yt2, chy2 = yt2r.next()
            S.dma("sp", lambda e, yt2=yt2, t0=t0, tt=tt: e.dma_start(
                out=yt2[:], in_=(y2_d(tt) if callable(y2_d) else y2_d[:, ycol0 + t0:ycol0 + t0 + 512]).rearrange("(c p) t -> p c t", p=128)),
                writes=[yt2], chan=chy2)
            ys = C.pp[:, PPL.sl("ysel")]
            TS(S, "dve", (yt, yt[:]), (yt, yt[:]), ys[:, 0:1], sb=[C.pp])
            STT(S, (yt, yt[:]), (yt2, yt2[:]), ys[:, 1:2], (yt, yt[:]), ALU.mult, ALU.add, sb=[C.pp])
        sq, _ = sqr.next()
        for c in range(4):
            ACT(S, (sq, sq[:, c, :]), (yt, yt[:, ssm_pos[c], :]), AF.Square)
        pss = S.next_ps()

        def fn(e, pss=pss, sq=sq):
            r = None
            for c in range(4):
                r = e.matmul(pss[:], C.ones_bf[:], sq[:, c, :], start=(c == 0), stop=(c == 3))
            return r
        S.op("pe", fn, reads=[sq, C.ones_bf], writes=[pss])
        lnv, _ = rsr.next()
        ACT(S, (lnv, lnv[:]), (pss, pss[:]), AF.Ln, bias=(C.epsc, C.epsc[:]), scale=1.0 / 512)
        S.free_ps(pss)
        rstd, _ = rsr.next()
        ACT(S, (rstd, rstd[:]), (lnv, lnv[:]), AF.Exp, scale=-0.5)
        for c in range(4):
            STT(S, (yt, yt[:, ssm_pos[c], :]), (yt, yt[:, ssm_pos[c], :]), ssmn[:, c:c + 1], (rstd, rstd[:]),
                ALU.mult, ALU.mult, sb=[C.pp])
        for dc in range(8):
            res, ch = resr.next()
            S.dma("sp", lambda e, res=res, dc=dc, t0=t0: e.dma_start(
                out=res[:], in_=hsrc[dc * 128:(dc + 1) * 128, t0:t0 + 512]), writes=[res], chan=ch)
            po = S.next_ps()
            S.op("pe", mm_group(S, po, po[:], [(wob[:, fc, dc * 128:(dc + 1) * 128], yt[:, fc, :]) for fc in range(12)]),
                 reads=[wob, yt], writes=[po])
            ot, ch2 = outr.next()
            STT(S, (ot, ot[:]), (po, po[:]), C.Gsc[:, dc:dc + 1], (res, res[:]), ALU.mult, ALU.add, sb=[C.Gsc])
            S.free_ps(po)
            S.dma("pool", lambda e, ot=ot, dc=dc, t0=t0: e.dma_start(
                out=hdst[dc * 128:(dc + 1) * 128, t0:t0 + 512], in_=ot[:]), reads=[ot], chan=ch2)
    S.barrier()


def final_phase(S, C, hsrc, odst, ntok):
    S.release()
    R = Ctx()
    R.htile = S.ring("ht", 2, (8, 512))
    R.sq = S.ring("sq", 1, (8, 512), BF16)
    R.rs = S.ring("rs", 4, (512,))
    R.tmp = S.ring("tmp", 3, (512,))
    outr = S.ring("fo", 2, (8, 512))
    for tt in range(ntok // 512):
        t0 = tt * 512
        ot, ch = outr.next()
        norm_mod_tile(S, C, R, hsrc[:, t0:t0 + 512], ot[:], ot)
        S.dma("pool", lambda e, ot=ot, t0=t0: e.dma_start(
            out=odst[:, t0:t0 + 512].rearrange("(c p) t -> p c t", p=128), in_=ot[:]), reads=[ot], chan=ch)
    S.barrier()


def mod_setup_norm(S, C, adaw_d, j0, adab_ap, gain_ap, nj=2):
    compute_mod(S, C, adaw_d, 0, j0, nj, adab_ap)
    S.op("dve", lambda e: e.scalar_tensor_tensor(out=C.Asc[:], in0=C.modv[:, 8:16], scalar=1.0,
                                                 in1=gain_ap, op0=ALU.add, op1=ALU.mult),
         reads=[C.modv, C.pp], writes=[C.Asc])


def mod_setup_gate(S, C, adaw_d, j, adab_ap, half):
    S.release()
    mstage = S.ring("mstg", 3, (2048,))
    pm = S.next_ps()
    blocked = len(adaw_d.shape) == 3
    wv = None if blocked else adaw_d.rearrange("(kc p) f -> p kc f", p=128)
    for blk in range(4):
        col0 = j * 1024 + blk * 256
        st, ch = mstage.next()
        sv = st[:, 0:2048].rearrange("p (kc f) -> p kc f", kc=8, f=256)
        src = (adaw_d[col0 // 256].rearrange("p (kc f) -> p kc f", kc=8, f=256) if blocked
               else wv[:, :, col0:col0 + 256])
        S.dma("sp", lambda e, sv=sv, src=src: e.dma_start(out=sv, in_=src),
              writes=[st], chan=ch)
        for cc in range(2):
            oc = blk * 2 + cc

            def fn(e, sv=sv, cc=cc, oc=oc):
                r = None
                for kc in range(8):
                    r = e.matmul(pm[:, oc:oc + 1], sv[:, kc, cc * 128:(cc + 1) * 128],
                                 C.cond[:, kc:kc + 1], start=(kc == 0), stop=(kc == 7))
                return r
            S.op("pe", fn, reads=[st, C.cond], writes=[pm])
    S.op("dve", lambda e: e.tensor_tensor(out=C.modv[:, 16:24], in0=pm[:, 0:8], in1=adab_ap, op=ALU.add),
         reads=[pm, C.pp], writes=[C.modv])
    S.free_ps(pm)
    S.op("dve", lambda e: e.tensor_scalar(out=C.Gsc[:], in0=C.modv[:, 16:24], scalar1=1.0,
                                          scalar2=(0.5 if half else 1.0), op0=ALU.add, op1=ALU.mult),
         reads=[C.modv], writes=[C.Gsc])
    S.barrier()


YROW_OWN = {"ret0": 0, "ret1": 128, "sb0": 256, "sb1": 384, "ssm0": 512, "ssm1": 640}


def build_B(hh, ntiles=16, dbg=9):
    nc = bass.Bass("TRN2", target_bir_lowering=False)
    ns = ntiles * 512
    hfull = nc.dram_tensor("hfull", [D, ns], F32, kind="ExternalInput").ap()
    pp_d = nc.dram_tensor("pp", [128, PPL.n], F32, kind="ExternalInput").ap()
    adaw = nc.dram_tensor("adaw", [D, 9 * D], F32, kind="ExternalInput").ap()
    win = nc.dram_tensor("win", [D, WCOLS], F32, kind="ExternalInput").ap()
    cst_d = nc.dram_tensor("cst", [128, CL.N], F32, kind="ExternalInput").ap()
    cstb_d = nc.dram_tensor("cstb", [128, CL.NB16], BF16, kind="ExternalInput").ap()
    rope_d = nc.dram_tensor("rope", [2, 128, SEQ], F32, kind="ExternalInput").ap()
    yown = nc.dram_tensor("yown", [768, ns], BF16, kind="ExternalOutput").ap()
    qs_d = nc.dram_tensor("qs_d", [256, ns], BF16).ap()
    kt_d = nc.dram_tensor("kt_d", [256, ns], BF16).ap()
    v_d = nc.dram_tensor("v_d", [ns, 256], BF16).ap()
    with contextlib.ExitStack() as es:
        S = Sched(nc, es)
        C = Ctx()
        setup_common(S, C, pp_d)
        mod_setup_norm(S, C, adaw, 3, C.pp[:, PPL.sl("adab_0", 24, 40)], C.pp[:, PPL.sl("g_mix_0")])
        mixer1_phase(S, C, 0, hh, hfull, win, cst_d, cstb_d, rope_d, qs_d, kt_d, v_d, yown, YROW_OWN, ntiles, dbg)
        if dbg >= 4:
            mixer2_phase(S, C, qs_d, kt_d, v_d, cst_d, cstb_d, yown, YROW_OWN, ntiles)
        S.emit()
    return nc


def build_C(last, ntok=NTOK):
    nc = bass.Bass("TRN2", target_bir_lowering=False)
    hin = nc.dram_tensor("hin", [D, ntok], F32, kind="ExternalInput").ap()
    y = nc.dram_tensor("y", [1536, ntok], BF16, kind="ExternalInput").ap()
    pp_d = nc.dram_tensor("pp", [128, PPL.n], F32, kind="ExternalInput").ap()
    adaw = nc.dram_tensor("adaw", [D, 9 * D], F32, kind="ExternalInput").ap()
    wout = nc.dram_tensor("wout", [1536, D], F32, kind="ExternalInput").ap()
    wg = nc.dram_tensor("wg", [D, DFF], F32, kind="ExternalInput").ap()
    wu = nc.dram_tensor("wu", [D, DFF], F32, kind="ExternalInput").ap()
    wd = nc.dram_tensor("wd", [DFF, D], F32, kind="ExternalInput").ap()
    if last:
        adaw2 = nc.dram_tensor("adaw2", [D, 2 * D], F32, kind="ExternalInput").ap()
    else:
        adaw2 = nc.dram_tensor("adaw2", [D, 9 * D], F32, kind="ExternalInput").ap()
        wg2 = nc.dram_tensor("wg2", [D, DFF], F32, kind="ExternalInput").ap()
        wu2 = nc.dram_tensor("wu2", [D, DFF], F32, kind="ExternalInput").ap()
        wd2 = nc.dram_tensor("wd2", [DFF, D], F32, kind="ExternalInput").ap()
    hout = nc.dram_tensor("hout", [D, ntok], F32, kind="ExternalOutput").ap()
    hb = nc.dram_tensor("hb", [D, ntok], F32).ap()
    with contextlib.ExitStack() as es:
        S = Sched(nc, es)
        C = Ctx()
        setup_common(S, C, pp_d)
        mod_setup_gate(S, C, adaw, 5, C.pp[:, PPL.sl("adab_0", 40, 48)], False)
        wout_phase(S, C, 0, y, 0, wout, hin, hb, ntok)
        compute_mod(S, C, adaw, 0, 6, 3, C.pp[:, PPL.sl("adab_0", 48, 72)])
        mod_affine(S, C, C.pp[:, PPL.sl("g_ffn2_0")], True)
        ffn_phase(S, C, hb, hb, wg, wu, wd, ntok)
        if last:
            mod_setup_norm(S, C, adaw2, 0, C.pp[:, PPL.sl("finb")], C.pp[:, PPL.sl("g_fin")])
            final_phase(S, C, hb, hout, ntok)
        else:
            compute_mod(S, C, adaw2, 0, 0, 3, C.pp[:, PPL.sl("adab_1", 0, 24)])
            mod_affine(S, C, C.pp[:, PPL.sl("g_ffn1_1")], True)
            ffn_phase(S, C, hb, hout, wg2, wu2, wd2, ntok)
        S.emit()
    return nc


PAIRS = [[0, 1], [2, 3], [4, 5], [6, 7]]


class HT:
    def __init__(self, aps):
        self.aps = aps

    def __getitem__(self, key):
        rows, cols = key
        assert cols.start % 512 == 0 and cols.stop - cols.start == 512
        return self.aps[cols.start // 512][rows, :]

SSM_POS_G = (4, 5, 10, 11)


def wout_perm():
    a = np.arange(128)
    rows = []
    for r in range(2):
        rows += [(2 * r) * 128 + a, (2 * r + 1) * 128 + a, 512 + (2 * r) * 128 + a, 512 + (2 * r + 1) * 128 + a,
                 1024 + (2 * r) * 128 + a, 1024 + (2 * r + 1) * 128 + a]
    return np.concatenate(rows)


def build_fused():
    nc = bass.Bass("TRN2", target_bir_lowering=False)
    ext = lambda n, sh, dt=F32: nc.dram_tensor(n, sh, dt, kind="ExternalInput").ap()
    hin = ext("hin", [D, NTOK])
    pp_d = ext("pp", [128, PPL.n])
    cst_d = ext("cst", [128, CL.N])
    cstb_d = ext("cstb", [128, CL.NB16], BF16)
    rope_d = ext("rope", [2, 128, SEQ])
    W = []
    for l in range(2):
        W.append({k: ext("%s%d" % (k, l), sh) for k, sh in (
            ("adaw", [36, 128, 2048]), ("win", [D, WCOLS]), ("wout", [1536, D]),
            ("f1g", [11, 128, 2048]), ("f1u", [11, 128, 2048]), ("f1d", [8, 128, 2816]),
            ("f2g", [11, 128, 2048]), ("f2u", [11, 128, 2048]), ("f2d", [8, 128, 2816]))})
    fadaw = ext("fadaw", [8, 128, 2048])
    out = nc.dram_tensor("out", [D, NTOK], F32, kind="ExternalOutput").ap()
    hloc_t = [nc.dram_tensor("hloc%d" % i, [D, 512], F32) for i in range(8)]
    hg_t = [nc.dram_tensor("hg%d" % i, [2 * D, 512], F32) for i in range(8)]
    y_t = [nc.dram_tensor("yown%d" % i, [768, 512], BF16) for i in range(16)]
    yg_t = [nc.dram_tensor("yg%d" % i, [1536, 512], BF16) for i in range(16)]
    qs_d = nc.dram_tensor("qs_d", [256, SEQ], BF16).ap()
    kt_d = nc.dram_tensor("kt_d", [256, SEQ], BF16).ap()
    v_d = nc.dram_tensor("v_d", [SEQ, 256], BF16).ap()
    hloc = HT([t.ap() for t in hloc_t])
    hg = [t.ap() for t in hg_t]
    yo = [t.ap() for t in y_t]
    yg = [t.ap() for t in yg_t]

    def gather(S, src_t, dst_t):
        S.cc(lambda e: e.collective_compute("AllGather", ALU.bypass, replica_groups=PAIRS,
                                            ins=[src_t.ap().opt()], outs=[dst_t.ap().opt()]), "cc")

    hsrc_tile = lambda g: hg[g % 8][(g // 8) * D:(g // 8 + 1) * D, :]
    ydst_tile = lambda g: (yo[g], 0)
    with contextlib.ExitStack() as es:
        S = Sched(nc, es)
        C = Ctx()
        setup_common(S, C, pp_d)
        compute_mod(S, C, W[0]["adaw"], 0, 0, 3, C.pp[:, PPL.sl("adab_0", 0, 24)])
        mod_affine(S, C, C.pp[:, PPL.sl("g_ffn1_0")], True)
        ffn_phase(S, C, hin, hloc, W[0]["f1g"], W[0]["f1u"], W[0]["f1d"], NTOK)
        for l in range(2):
            w = W[l]
            for i in range(8):
                gather(S, hloc_t[i], hg_t[i])
            S.barrier()
            mod_setup_norm(S, C, w["adaw"], 3, C.pp[:, PPL.sl("adab_%d" % l, 24, 40)], C.pp[:, PPL.sl("g_mix_%d" % l)])
            mixer1_phase(S, C, l, 0, hsrc_tile, w["win"], cst_d, cstb_d, rope_d, qs_d, kt_d, v_d, ydst_tile, YROW_OWN, 16)
            mixer2_phase(S, C, qs_d, kt_d, v_d, cst_d, cstb_d, ydst_tile, YROW_OWN, 16)
            for i in range(16):
                gather(S, y_t[i], yg_t[i])
            S.barrier()
            mod_setup_gate(S, C, w["adaw"], 5, C.pp[:, PPL.sl("adab_%d" % l, 40, 48)], False)
            wout_phase(S, C, l, (lambda tt: yg[tt]), 0, w["wout"], hloc, hloc, NTOK,
                       y2_d=(lambda tt: yg[8 + tt]), ssm_pos=SSM_POS_G)
            compute_mod(S, C, w["adaw"], 0, 6, 3, C.pp[:, PPL.sl("adab_%d" % l, 48, 72)])
            mod_affine(S, C, C.pp[:, PPL.sl("g_ffn2_%d" % l)], True)
            ffn_phase(S, C, hloc, hloc, w["f2g"], w["f2u"], w["f2d"], NTOK)
            if l == 0:
                compute_mod(S, C, W[1]["adaw"], 0, 0, 3, C.pp[:, PPL.sl("adab_1", 0, 24)])
                mod_affine(S, C, C.pp[:, PPL.sl("g_ffn1_1")], True)
                ffn_phase(S, C, hloc, hloc, W[1]["f1g"], W[1]["f1u"], W[1]["f1d"], NTOK)
            else:
                mod_setup_norm(S, C, fadaw, 0, C.pp[:, PPL.sl("finb")], C.pp[:, PPL.sl("g_fin")])
                final_phase(S, C, hloc, out, NTOK)
        S.emit()
    return nc


_PROG = {}


def kernel(**inp):
    inp = {k: np.asarray(v) for k, v in inp.items()}
    f32 = lambda a: np.ascontiguousarray(a, dtype=np.float32)
    if "F" not in _PROG:
        _PROG["F"] = build_fused()
    nc = _PROG["F"]
    rope = make_rope()
    consts = [make_consts(hf) for hf in range(2)]
    perm = wout_perm()
    ab = lambda w: f32(np.asarray(w).reshape(8, 128, -1, 256).transpose(2, 1, 0, 3).reshape(-1, 128, 2048))
    shared = {"rope": rope, "fadaw": ab(inp["final_ada_w"])}
    gu = lambda w: f32(np.asarray(w).reshape(8, 128, 11, 256).transpose(2, 1, 0, 3).reshape(11, 128, 2048))
    dn = lambda w: f32(np.asarray(w).reshape(NFC, 128, 8, 128).transpose(2, 1, 0, 3).reshape(8, 128, 2816))
    for l in range(2):
        shared.update({"adaw%d" % l: ab(inp["ada_w"][l]), "wout%d" % l: f32(inp["w_out"][l][perm]),
                       "f1g%d" % l: gu(inp["ffn1_wg"][l]), "f1u%d" % l: gu(inp["ffn1_wu"][l]),
                       "f1d%d" % l: dn(inp["ffn1_wd"][l]), "f2g%d" % l: gu(inp["ffn2_wg"][l]),
                       "f2u%d" % l: gu(inp["ffn2_wu"][l]), "f2d%d" % l: dn(inp["ffn2_wd"][l])})
    wins = [[f32(inp["w_in"][l][:, win_cols(hf)]) for hf in range(2)] for l in range(2)]
    maps = []
    for b in range(NB):
        xT = f32(inp["x"][b].T)
        for hf in range(2):
            m = dict(shared)
            m.update({"hin": f32(xT[:, hf * NTOK:(hf + 1) * NTOK]), "pp": pack_pp(inp, b, hf, (0, 1)),
                      "cst": consts[hf][0], "cstb": consts[hf][1], "win0": wins[0][hf], "win1": wins[1][hf]})
            maps.append(m)
    res = run_bass_kernel_spmd(nc, maps, core_ids=list(range(8))).results
    o = np.empty((NB, SEQ, D), np.float32)
    for b in range(NB):
        o[b] = np.concatenate([res[2 * b]["out"], res[2 * b + 1]["out"]], axis=1).T
    return o
```

```python
import contextlib
import math
import numpy as np
import concourse.bass as bass
import concourse.mybir as mybir
from concourse.bass_utils import run_bass_kernel_spmd

F32 = mybir.dt.float32
F32R = mybir.dt.float32r
BF16 = mybir.dt.bfloat16
ALU = mybir.AluOpType
AF = mybir.ActivationFunctionType

D = 1024
NB = 4
SEQ = 8192
NTOK = 4096
DFF = 2816
NFC = 22
EPS = 1e-6
INW = 5128
ENGS = ("pe", "act", "dve", "pool", "sp")
DT_SIZE = {F32: 4, F32R: 4, BF16: 2}


class Buf:
    __slots__ = ("name", "t", "last_w", "readers", "excl", "last_acc")

    def __init__(self, name, t, excl=False):
        self.name = name
        self.t = t
        self.last_w = None
        self.readers = []
        self.excl = excl
        self.last_acc = []

    def __getitem__(self, k):
        return self.t[k]


class Op:
    __slots__ = ("idx", "eng", "fn", "deps", "is_dma", "chan", "n_dma", "val",
                 "needs_inc", "barrier", "bvals", "desc", "unit")


class Sched:
    def __init__(self, nc, es, arena_f32=52000):
        self.nc = nc
        self.es = es
        self.ops = []
        self.streams = {e: [] for e in ENGS}
        self.chan_last = {}
        self.chan_count = {}
        self.bufs = []
        self.arena = es.enter_context(nc.sbuf_tensor("arena", [128, arena_f32], F32))
        self.cap = arena_f32 * 4
        self.top = 0
        self.base = 0
        self.psum = [self._ps("ps%d" % i) for i in range(8)]
        self.ps_i = 0
        self.ps_free = {}
        self.inst_map = {}

    def _ps(self, name):
        t = self.es.enter_context(self.nc.psum_tensor(name, [128, 512], F32))
        b = Buf(name, t, excl=True)
        self.bufs.append(b)
        return b

    def next_ps(self):
        for _ in range(8):
            b = self.psum[self.ps_i % 8]
            self.ps_i += 1
            if self.ps_free.get(b.name, True):
                self.ps_free[b.name] = False
                return b
        raise AssertionError("all PSUM banks live")

    def free_ps(self, b):
        self.ps_free[b.name] = True

    def alloc(self, name, free_shape, dt=F32):
        free_shape = tuple(int(x) for x in free_shape)
        n = int(np.prod(free_shape))
        nb = (n * DT_SIZE[dt] + 63) // 64 * 64
        off = self.top
        self.top += nb
        assert self.top <= self.cap, ("SBUF arena overflow", name, self.top, self.cap)
        ap = self.arena[:, off // 4:(off + nb) // 4]
        if dt != F32:
            ap = ap.bitcast(dt)
        ap = ap[:, :n]
        if len(free_shape) == 2:
            ap = ap.rearrange("p (a b) -> p a b", a=free_shape[0], b=free_shape[1])
        elif len(free_shape) == 3:
            ap = ap.rearrange("p (a b c) -> p a b c", a=free_shape[0], b=free_shape[1],
                              c=free_shape[2])
        b = Buf(name, ap)
        self.bufs.append(b)
        return b

    def ring(self, name, k, free_shape, dt=F32):
        return Ring([self.alloc("%s%d" % (name, i), free_shape, dt) for i in range(k)], name)

    def mark(self):
        self.base = self.top

    def release(self):
        self.top = self.base

    def _record(self, eng, fn, reads, writes, is_dma, chan, n_dma, unit=16):
        op = Op()
        op.unit = unit
        op.idx = len(self.ops)
        op.eng = eng
        op.fn = fn
        op.is_dma = is_dma
        op.chan = chan
        op.n_dma = n_dma
        op.val = None
        op.needs_inc = False
        op.barrier = False
        deps = {}

        def add(d, kind):
            if d is None:
                return
            if kind == "raw" or d not in deps:
                deps[d] = kind

        for b in reads:
            add(b.last_w, "raw")
            if b.excl:
                for a in b.last_acc:
                    add(a, "war")
        for b in writes:
            add(b.last_w, "waw")
            for r in b.readers:
                add(r, "war")
            if b.excl:
                for a in b.last_acc:
                    add(a, "war")
        if is_dma and chan in self.chan_last:
            add(self.chan_last[chan], "raw")
        for b in reads:
            if b not in writes:
                b.readers.append(op.idx)
            if b.excl:
                b.last_acc = [op.idx]
        for b in writes:
            b.last_w = op.idx
            b.readers = []
            if b.excl:
                b.last_acc = [op.idx]
        if is_dma:
            self.chan_last[chan] = op.idx
            self.chan_count[chan] = self.chan_count.get(chan, 0) + unit * n_dma
            op.val = self.chan_count[chan]
        op.deps = deps
        op.desc = "R[%s] W[%s]" % (",".join(b.name for b in reads), ",".join(b.name for b in writes))
        self.ops.append(op)
        self.streams[eng].append(op)
        return op

    def op(self, eng, fn, reads=(), writes=()):
        return self._record(eng, fn, list(reads), list(writes), False, None, 0)

    def dma(self, eng, fn, reads=(), writes=(), chan=None, n=1):
        assert chan is not None
        return self._record(eng, fn, list(reads), list(writes), True, chan, n)

    def cc(self, fn, chan):
        return self._record("pool", fn, [], [], True, chan, 1, unit=1)

    def barrier(self):
        bops = []
        for e in ENGS:
            op = Op()
            op.idx = len(self.ops)
            op.eng = e
            op.fn = None
            op.is_dma = False
            op.chan = None
            op.n_dma = 0
            op.val = None
            op.needs_inc = False
            op.barrier = True
            op.unit = 0
            op.deps = {}
            op.bvals = dict(self.chan_count)
            self.ops.append(op)
            self.streams[e].append(op)
            bops.append(op)
        for b in self.bufs:
            b.last_w = None
            b.readers = []
            b.last_acc = []
        self.chan_last = {}

    def emit(self):
        nc = self.nc
        ops = self.ops
        need = {}
        for op in ops:
            if op.barrier:
                continue
            w = []
            for d, kind in op.deps.items():
                p = ops[d]
                if p.is_dma:
                    w.append(d)
                    continue
                if p.eng == op.eng and not op.is_dma:
                    if kind != "raw" or op.eng == "pe":
                        continue
                w.append(d)
                p.needs_inc = True
            need[op.idx] = w
        for e in ENGS:
            last = None
            for op in self.streams[e]:
                if op.barrier:
                    if last is not None:
                        last.needs_inc = True
                elif not op.is_dma:
                    last = op
        ecount_at = {}
        for e in ENGS:
            c = 0
            for op in self.streams[e]:
                if op.barrier:
                    ecount_at[(e, op.idx)] = c
                elif not op.is_dma and op.needs_inc:
                    c += 1
                    op.val = c
        es = self.es
        esem = {e: es.enter_context(nc.semaphore("s_" + e)) for e in ENGS}
        csem = {c: es.enter_context(nc.semaphore("c_%s" % (c,))) for c in self.chan_count}
        self.n_sems = len(esem) + len(csem)
        block = es.enter_context(nc.Block())
        bar_groups = {}
        for op in ops:
            if op.barrier:
                g = op.idx - ENGS.index(op.eng)
                bar_groups.setdefault(g, {})[op.eng] = op

        plan = {}
        for ename in ENGS:
            waited = {}
            lst = []

            def wl(key, val, acc):
                if val > 0 and waited.get(key, 0) < val:
                    acc.append((key, val))
                    waited[key] = val

            for op in self.streams[ename]:
                acc = []
                if op.barrier:
                    g = op.idx - ENGS.index(ename)
                    for e2 in ENGS:
                        wl(("e", e2), ecount_at[(e2, bar_groups[g][e2].idx)], acc)
                    for c, v in op.bvals.items():
                        wl(("c", c), v, acc)
                    lst.append((acc, None))
                    continue
                for d in need[op.idx]:
                    p = ops[d]
                    wl(("c", p.chan) if p.is_dma else ("e", p.eng), p.val, acc)
                lst.append((acc, op))
            if ename == "sp":
                acc = []
                for c, tot in self.chan_count.items():
                    wl(("c", c), tot, acc)
                lst.append((acc, None))
            plan[ename] = lst
        semv = {}
        pos = {e: 0 for e in ENGS}
        progress = True
        while progress:
            progress = False
            for e in ENGS:
                while pos[e] < len(plan[e]):
                    acc, op = plan[e][pos[e]]
                    if any(semv.get(k, 0) < v for k, v in acc):
                        break
                    if op is not None:
                        if op.is_dma:
                            semv[("c", op.chan)] = semv.get(("c", op.chan), 0) + op.unit * op.n_dma
                        elif op.needs_inc:
                            semv[("e", e)] = semv.get(("e", e), 0) + 1
                    pos[e] += 1
                    progress = True
        for e in ENGS:
            if pos[e] < len(plan[e]):
                acc, op = plan[e][pos[e]]
                raise AssertionError("semaphore deadlock: engine %s stuck at %d/%d waiting %s (have %s)" % (
                    e, pos[e], len(plan[e]), acc, [(k, semv.get(k, 0)) for k, v in acc]))
        self.plan_sizes = {e: len(plan[e]) for e in ENGS}

        def semof(key):
            return csem[key[1]] if key[0] == "c" else esem[key[1]]

        def run_stream(ename):
            def body(eng):
                for acc, op in plan[ename]:
                    for key, val in acc:
                        eng.wait_ge(semof(key), val)
                    if op is None:
                        continue
                    r = op.fn(eng)
                    try:
                        rr_ = r[-1] if isinstance(r, (list, tuple)) else r
                        self.inst_map[str(rr_.ins.name)] = (ename, op.idx, op.desc)
                    except Exception:
                        pass
                    if op.is_dma:
                        rs = r if isinstance(r, (list, tuple)) else [r]
                        assert len(rs) == op.n_dma, (len(rs), op.n_dma)
                        for i in rs:
                            i.then_inc(csem[op.chan], op.unit)
                    elif op.needs_inc:
                        r.then_inc(esem[ename], 1)
            return body

        block.tensor(run_stream("pe"))
        block.scalar(run_stream("act"))
        block.vector(run_stream("dve"))
        block.gpsimd(run_stream("pool"))
        block.sync(run_stream("sp"))


class Ring:
    def __init__(self, bufs, name):
        self.bufs = bufs
        self.i = 0
        self.name = name

    def next(self):
        k = self.i % len(self.bufs)
        self.i += 1
        return self.bufs[k], "%s_%d" % (self.name, k)


def mm_group(S, out_buf, out_ap, pairs, extra_reads=()):
    n = len(pairs)

    def fn(e):
        r = None
        for i, (l, rr) in enumerate(pairs):
            r = e.matmul(out_ap, l, rr, start=(i == 0), stop=(i == n - 1))
        return r
    return fn


def chunked(v):
    v = np.asarray(v, np.float32)
    return np.ascontiguousarray(v.reshape(-1, 128).T)


class PP:
    def __init__(self):
        self.cols = {}
        self.n = 0

    def add(self, name, w):
        self.cols[name] = (self.n, w)
        self.n += w

    def sl(self, name, a=0, b=None):
        o, w = self.cols[name]
        if b is None:
            b = w
        return slice(o + a, o + b)


def pp_layout():
    P = PP()
    P.add("c", 8)
    for l in range(2):
        P.add("g_ffn1_%d" % l, 8)
        P.add("g_mix_%d" % l, 8)
        P.add("g_ffn2_%d" % l, 8)
        P.add("adab_%d" % l, 72)
        P.add("convw_%d" % l, 16)
        P.add("convb_%d" % l, 4)
        P.add("dtb_%d" % l, 4)
        P.add("alog_%d" % l, 4)
        P.add("dsk_%d" % l, 2)
        P.add("retgn_%d" % l, 2)
        P.add("ssmn_%d" % l, 4)
    P.add("g_fin", 8)
    P.add("finb", 16)
    P.add("ysel", 2)
    return P


PPL = pp_layout()


def pack_pp(inp, b, hh, lmap=(0, 1)):
    P = PPL
    a = np.zeros((128, P.n), np.float32)
    a[:, P.sl("c")] = chunked(inp["c"][b])
    for slot, L in enumerate(lmap):
        l = slot
        inp_l = {k: inp[k][L] for k in ("norm_ffn1", "norm_mix", "norm_ffn2", "ada_b", "conv_w", "conv_b", "dt_bias", "a_log", "d_skip", "ret_gn", "ssm_norm")}
        a[:, P.sl("g_ffn1_%d" % l)] = chunked(inp_l["norm_ffn1"])
        a[:, P.sl("g_mix_%d" % l)] = chunked(inp_l["norm_mix"])
        a[:, P.sl("g_ffn2_%d" % l)] = chunked(inp_l["norm_ffn2"])
        a[:, P.sl("adab_%d" % l)] = chunked(inp_l["ada_b"])
        ch = np.concatenate([np.arange(hh * 256, hh * 256 + 256),
                             512 + hh * 128 + np.arange(128),
                             768 + hh * 128 + np.arange(128)])
        cw = inp_l["conv_w"][:, ch]
        a[:, P.sl("convw_%d" % l)] = cw.reshape(4, 4, 128).transpose(2, 1, 0).reshape(128, 16)
        a[:, P.sl("convb_%d" % l)] = chunked(inp_l["conv_b"][ch])
        hs = np.arange(4 * hh, 4 * hh + 4)
        a[:, P.sl("dtb_%d" % l)] = np.broadcast_to(inp_l["dt_bias"][hs], (128, 4))
        a[:, P.sl("alog_%d" % l)] = np.broadcast_to(inp_l["a_log"][hs], (128, 4))
        dsk = inp_l["d_skip"][hs]
        a[:, P.sl("dsk_%d" % l)] = np.stack(
            [np.repeat(dsk[0:2], 64), np.repeat(dsk[2:4], 64)], axis=1)
        a[:, P.sl("retgn_%d" % l)] = chunked(inp_l["ret_gn"][hh * 256:hh * 256 + 256])
        a[:, P.sl("ssmn_%d" % l)] = chunked(inp_l["ssm_norm"])
    a[:, P.sl("g_fin")] = chunked(inp["final_norm"])
    a[:, P.sl("finb")] = chunked(inp["final_ada_b"])
    a[:, P.sl("ysel")] = np.array([1.0, 0.0] if hh == 0 else [0.0, 1.0], np.float32)[None, :]
    return a


class Ctx:
    pass


def setup_common(S, C, pp_d):
    C.pp = S.alloc("pp", (PPL.n,))
    S.dma("sp", lambda e: e.dma_start(out=C.pp[:], in_=pp_d), writes=[C.pp], chan="pp")
    C.cond = S.alloc("cond", (8,))
    S.op("act", lambda e: e.activation(out=C.cond[:], in_=C.pp[:, PPL.sl("c")], func=AF.Silu),
         reads=[C.pp], writes=[C.cond])
    C.ones_bf = S.alloc("ones_bf", (128,), BF16)
    S.op("pool", lambda e: e.memset(C.ones_bf[:], 1.0), writes=[C.ones_bf])
    C.epsc = S.alloc("epsc", (1,))
    S.op("pool", lambda e: e.memset(C.epsc[:], EPS), writes=[C.epsc])
    C.onec = S.alloc("onec", (1,))
    S.op("pool", lambda e: e.memset(C.onec[:], 1.0), writes=[C.onec])
    C.modv = S.alloc("modv", (24,))
    C.Asc = S.alloc("Asc", (8,))
    C.Gsc = S.alloc("Gsc", (8,))
    S.mark()


def compute_mod(S, C, adaw_d, ncols_total, j0, nj, adab_ap):
    S.release()
    mstage = S.ring("mstg", 3, (2048,))
    pm = S.next_ps()
    blocked = len(adaw_d.shape) == 3
    wv = None if blocked else adaw_d.rearrange("(kc p) f -> p kc f", p=128)
    for blk in range(nj * 4):
        col0 = j0 * 1024 + blk * 256
        st, ch = mstage.next()
        sv = st[:, 0:2048].rearrange("p (kc f) -> p kc f", kc=8, f=256)
        src = (adaw_d[col0 // 256].rearrange("p (kc f) -> p kc f", kc=8, f=256) if blocked
               else wv[:, :, col0:col0 + 256])
        S.dma("sp", lambda e, sv=sv, src=src: e.dma_start(out=sv, in_=src),
              writes=[st], chan=ch)
        for cc in range(2):
            oc = blk * 2 + cc

            def fn(e, sv=sv, cc=cc, oc=oc):
                r = None
                for kc in range(8):
                    r = e.matmul(pm[:, oc:oc + 1], sv[:, kc, cc * 128:(cc + 1) * 128],
                                 C.cond[:, kc:kc + 1], start=(kc == 0), stop=(kc == 7))
                return r
            S.op("pe", fn, reads=[st, C.cond], writes=[pm])
    n = nj * 8
    S.op("dve", lambda e: e.tensor_tensor(out=C.modv[:, 0:n], in0=pm[:, 0:n], in1=adab_ap,
                                          op=ALU.add), reads=[pm, C.pp], writes=[C.modv])
    S.free_ps(pm)
    S.barrier()


def mod_affine(S, C, gain_ap, gate_half):
    S.op("dve", lambda e: e.scalar_tensor_tensor(out=C.Asc[:], in0=C.modv[:, 8:16], scalar=1.0,
                                                 in1=gain_ap, op0=ALU.add, op1=ALU.mult),
         reads=[C.modv, C.pp], writes=[C.Asc])
    S.op("dve", lambda e: e.tensor_scalar(out=C.Gsc[:], in0=C.modv[:, 16:24], scalar1=1.0,
                                          scalar2=(0.5 if gate_half else 1.0),
                                          op0=ALU.add, op1=ALU.mult),
         reads=[C.modv], writes=[C.Gsc])


def load_h_tile(S, R, src_ap):
    ht, ch = R.htile.next()
    S.dma("sp", lambda e: e.dma_start(out=ht[:], in_=src_ap.rearrange("(c p) t -> p c t", p=128)),
          writes=[ht], chan=ch)
    return ht


def norm_mod_tile(S, C, R, src_ap, un_ap, un_buf, ht=None):
    if ht is None:
        ht = load_h_tile(S, R, src_ap)
    sq, _ = R.sq.next()
    S.op("act", lambda e: e.activation(out=sq[:], in_=ht[:], func=AF.Square), reads=[ht], writes=[sq])
    pss = S.next_ps()

    def fn(e):
        r = None
        for c in range(8):
            r = e.matmul(pss[:], C.ones_bf[:], sq[:, c, :], start=(c == 0), stop=(c == 7))
        return r
    S.op("pe", fn, reads=[sq, C.ones_bf], writes=[pss])
    lnv, _ = R.rs.next()
    S.op("act", lambda e: e.activation(out=lnv[:], in_=pss[:], func=AF.Ln, bias=C.epsc[:],
                                       scale=1.0 / D), reads=[pss, C.epsc], writes=[lnv])
    S.free_ps(pss)
    rstd, _ = R.rs.next()
    S.op("act", lambda e: e.activation(out=rstd[:], in_=lnv[:], func=AF.Exp, scale=-0.5),
         reads=[lnv], writes=[rstd])
    for c in range(8):
        tmp, _ = R.tmp.next()
        S.op("dve", lambda e, c=c, tmp=tmp: e.scalar_tensor_tensor(
            out=tmp[:], in0=ht[:, c, :], scalar=C.Asc[:, c:c + 1], in1=rstd[:],
            op0=ALU.mult, op1=ALU.mult), reads=[ht, C.Asc, rstd], writes=[tmp])
        S.op("act", lambda e, c=c, tmp=tmp: e.activation(
            out=un_ap[:, c, :], in_=tmp[:], func=AF.Identity, bias=C.modv[:, c:c + 1], scale=1.0),
            reads=[tmp, C.modv], writes=[un_buf])


def load_cast(S, C, dram_view, shape3, wring, stage, eng="act"):
    a, b = shape3
    st, ch = stage.next()
    sv = st[:, 0:a * b].rearrange("p (a b) -> p a b", a=a, b=b)
    S.dma("sp", lambda e: e.dma_start(out=sv, in_=dram_view), writes=[st], chan=ch)
    wb, _ = wring.next()
    CP(S, eng, (wb, wb[:]), (st, sv))
    return wb


def ffn_phase(S, C, hsrc, hdst, wg_d, wu_d, wd_d, ntok):
    TT = 1024
    NTT = TT // 512
    R = Ctx()
    S.release()
    un = S.alloc("un", (8, TT), BF16)
    hid = S.alloc("hid", (NFC, TT), BF16)
    R.htile = S.ring("ht", 1, (8, 512))
    R.sq = S.ring("sq", 1, (8, 512), BF16)
    R.rs = S.ring("rs", 4, (512,))
    R.tmp = S.ring("tmp", 3, (512,))
    wgr = S.ring("wgb", 2, (8, 256), BF16)
    wur = S.ring("wub", 2, (8, 256), BF16)
    wdr = S.ring("wdb", 2, (NFC, 128), BF16)
    sgr = S.ring("sg", 3, (512,), BF16)
    resr = S.ring("res", 3, (512,))
    outr = S.ring("outt", 3, (512,))
    stage = S.ring("stg", 3, (2816,))
    blocked = len(wg_d.shape) == 3
    if blocked:
        wg_blk = lambda fp: wg_d[fp].rearrange("p (a b) -> p a b", a=8, b=256)
        wu_blk = lambda fp: wu_d[fp].rearrange("p (a b) -> p a b", a=8, b=256)
        wd_blk = lambda dc: wd_d[dc].rearrange("p (a b) -> p a b", a=NFC, b=128)
    else:
        wgv = wg_d.rearrange("(kc p) f -> p kc f", p=128)
        wuv = wu_d.rearrange("(kc p) f -> p kc f", p=128)
        wdv = wd_d.rearrange("(fc p) d -> p fc d", p=128)
        wg_blk = lambda fp: wgv[:, :, fp * 256:(fp + 1) * 256]
        wu_blk = lambda fp: wuv[:, :, fp * 256:(fp + 1) * 256]
        wd_blk = lambda dc: wdv[:, :, dc * 128:(dc + 1) * 128]
    uns = [un, S.alloc("un_b", (8, TT), BF16)]
    nst = ntok // TT

    def p0(st_j):
        u = uns[st_j % 2]
        for tt in range(NTT):
            t0 = st_j * TT + tt * 512
            norm_mod_tile(S, C, R, hsrc[:, t0:t0 + 512], u[:, :, tt * 512:(tt + 1) * 512], u)

    p0(0)
    for st_i in range(nst):
        T0 = st_i * TT
        un = uns[st_i % 2]
        for fp in range(NFC // 2):
            wgb = load_cast(S, C, wg_blk(fp), (8, 256), wgr, stage, "act")
            wub = load_cast(S, C, wu_blk(fp), (8, 256), wur, stage, "dve")
            for tt in range(NTT):
                for j in range(2):
                    f = fp * 2 + j
                    pg = S.next_ps()
                    S.op("pe", mm_group(S, pg, pg[:], [
                        (wgb[:, kc, j * 128:(j + 1) * 128], un[:, kc, tt * 512:(tt + 1) * 512])
                        for kc in range(8)]), reads=[wgb, un], writes=[pg])
                    pu = S.next_ps()
                    S.op("pe", mm_group(S, pu, pu[:], [
                        (wub[:, kc, j * 128:(j + 1) * 128], un[:, kc, tt * 512:(tt + 1) * 512])
                        for kc in range(8)]), reads=[wub, un], writes=[pu])
                    sg, _ = sgr.next()
                    S.op("act", lambda e, sg=sg, pg=pg: e.activation(out=sg[:], in_=pg[:], func=AF.Silu),
                         reads=[pg], writes=[sg])
                    S.op("dve", lambda e, sg=sg, pu=pu, f=f, tt=tt: e.tensor_tensor(
                        out=hid[:, f, tt * 512:(tt + 1) * 512], in0=sg[:], in1=pu[:], op=ALU.mult),
                        reads=[sg, pu], writes=[hid])
                    S.free_ps(pg)
                    S.free_ps(pu)
        if st_i + 1 < nst:
            p0(st_i + 1)
        for dc in range(8):
            wdb = load_cast(S, C, wd_blk(dc), (NFC, 128), wdr, stage, "act" if dc % 2 == 0 else "dve")
            for tt in range(NTT):
                t0 = T0 + tt * 512
                res, ch = resr.next()
                S.dma("sp", lambda e, res=res, dc=dc, t0=t0: e.dma_start(
                    out=res[:], in_=hsrc[dc * 128:(dc + 1) * 128, t0:t0 + 512]), writes=[res], chan=ch)
                po = S.next_ps()
                S.op("pe", mm_group(S, po, po[:], [
                    (wdb[:, fc, :], hid[:, fc, tt * 512:(tt + 1) * 512]) for fc in range(NFC)]),
                    reads=[wdb, hid], writes=[po])
                ot, ch2 = outr.next()
                S.op("dve", lambda e, ot=ot, po=po, res=res, dc=dc: e.scalar_tensor_tensor(
                    out=ot[:], in0=po[:], scalar=C.Gsc[:, dc:dc + 1], in1=res[:],
                    op0=ALU.mult, op1=ALU.add), reads=[po, C.Gsc, res], writes=[ot])
                S.free_ps(po)
                S.dma("pool", lambda e, ot=ot, dc=dc, t0=t0: e.dma_start(
                    out=hdst[dc * 128:(dc + 1) * 128, t0:t0 + 512], in_=ot[:]), reads=[ot], chan=ch2)
    S.barrier()


def build_A(ntok=NTOK):
    nc = bass.Bass("TRN2", target_bir_lowering=False)
    hin = nc.dram_tensor("hin", [D, ntok], F32, kind="ExternalInput").ap()
    pp_d = nc.dram_tensor("pp", [128, PPL.n], F32, kind="ExternalInput").ap()
    adaw = nc.dram_tensor("adaw", [D, 9 * D], F32, kind="ExternalInput").ap()
    wg = nc.dram_tensor("wg", [D, DFF], F32, kind="ExternalInput").ap()
    wu = nc.dram_tensor("wu", [D, DFF], F32, kind="ExternalInput").ap()
    wd = nc.dram_tensor("wd", [DFF, D], F32, kind="ExternalInput").ap()
    hout = nc.dram_tensor("hout", [D, ntok], F32, kind="ExternalOutput").ap()
    with contextlib.ExitStack() as es:
        S = Sched(nc, es)
        C = Ctx()
        setup_common(S, C, pp_d)
        compute_mod(S, C, adaw, 9 * D, 0, 3, C.pp[:, PPL.sl("adab_0", 0, 24)])
        mod_affine(S, C, C.pp[:, PPL.sl("g_ffn1_0")], True)
        ffn_phase(S, C, hin, hout, wg, wu, wd, ntok)
        S.emit()
    return nc


def ACT(S, o, i, func, bias=None, scale=1.0, extra=()):
    ob, oap = o
    ib, iap = i
    rd = [ib] + list(extra)
    if bias is not None:
        rd.append(bias[0])
        S.op("act", lambda e: e.activation(out=oap, in_=iap, func=func, bias=bias[1], scale=scale),
             reads=rd, writes=[ob])
    else:
        S.op("act", lambda e: e.activation(out=oap, in_=iap, func=func, scale=scale),
             reads=rd, writes=[ob])


def TT(S, eng, o, a, b, op):
    S.op(eng, lambda e: e.tensor_tensor(out=o[1], in0=a[1], in1=b[1], op=op),
         reads=[a[0], b[0]], writes=[o[0]])


def TS(S, eng, o, a, s1, s2=None, op0=ALU.mult, op1=None, sb=()):
    if op1 is None:
        S.op(eng, lambda e: e.tensor_scalar(out=o[1], in0=a[1], scalar1=s1, scalar2=None, op0=op0),
             reads=[a[0]] + list(sb), writes=[o[0]])
    else:
        S.op(eng, lambda e: e.tensor_scalar(out=o[1], in0=a[1], scalar1=s1, scalar2=s2, op0=op0, op1=op1),
             reads=[a[0]] + list(sb), writes=[o[0]])


def STT(S, o, a, sc, b, op0, op1, sb=()):
    S.op("dve", lambda e: e.scalar_tensor_tensor(out=o[1], in0=a[1], scalar=sc, in1=b[1], op0=op0, op1=op1),
         reads=[a[0], b[0]] + list(sb), writes=[o[0]])


def CP(S, eng, o, a):
    if eng == "act":
        S.op("act", lambda e: e.activation(out=o[1], in_=a[1], func=AF.Copy), reads=[a[0]], writes=[o[0]])
    else:
        S.op(eng, lambda e: e.tensor_copy(out=o[1], in_=a[1]), reads=[a[0]], writes=[o[0]])


class CL:
    U = 0
    TRI = 128
    BLK = 256
    NEGM = 384
    ONES = 512
    DMAT = 640
    QDEC = 1664
    KDEC = 2688
    HMASK = 2692
    GL = 2694
    N = 2696
    IDENT = 0
    SBM = 128
    TRIB = 128 + 2048
    BLKB = 128 + 2048 + 128
    NB16 = 128 + 2048 + 256


def ret_gamma(hglob):
    return 1.0 - 2.0 ** (-5.0 - hglob)


def make_consts(hh):
    c = np.zeros((128, CL.N), np.float32)
    p = np.arange(128)
    j = p[:, None]
    s = p[None, :]
    same = (j // 64) == (s // 64)
    c[:, CL.U:CL.U + 128] = (j > s)
    c[:, CL.TRI:CL.TRI + 128] = same & (j <= s)
    c[:, CL.BLK:CL.BLK + 128] = same
    c[:, CL.NEGM:CL.NEGM + 128] = np.where(same & (s >= j), 0.0, -30000.0)
    c[:, CL.ONES:CL.ONES + 128] = 1.0
    sc = 128.0 ** -0.5
    for hl in range(2):
        lg = math.log1p(-(2.0 ** (-5.0 - (2 * hh + hl))))
        dm = np.where(same, sc * np.exp(lg * np.abs(j - s)), 0.0)
        c[:, CL.DMAT + hl * 512:CL.DMAT + (hl + 1) * 512] = np.tile(dm, (1, 4))
        l = np.arange(512)
        c[:, CL.QDEC + hl * 512:CL.QDEC + (hl + 1) * 512] = np.exp(lg * ((l % 64) + 1.0))[None, :]
        kd = sc * np.exp(lg * (63.0 - (p % 64)))
        c[:, CL.KDEC + hl * 2 + 0] = np.where(p < 64, kd, 0.0)
        c[:, CL.KDEC + hl * 2 + 1] = np.where(p >= 64, kd, 0.0)
    for hl in range(2):
        c[:, CL.GL + hl] = ret_gamma(2 * hh + hl) ** 64
    c[:, CL.HMASK + 0] = (p < 64)
    c[:, CL.HMASK + 1] = (p >= 64)
    import ml_dtypes
    b = np.zeros((128, CL.NB16), np.float32)
    b[:, CL.IDENT:CL.IDENT + 128] = np.eye(128)
    t = np.arange(512)[None, :]
    for i in range(4):
        b[:, CL.SBM + i * 512:CL.SBM + (i + 1) * 512] = ((i * 128 + j) < t)
    b[:, CL.TRIB:CL.TRIB + 128] = c[:, CL.TRI:CL.TRI + 128]
    b[:, CL.BLKB:CL.BLKB + 128] = c[:, CL.BLK:CL.BLK + 128]
    return c, b.astype(ml_dtypes.bfloat16)


def make_rope():
    half = 64
    inv = (10000.0 ** (-np.arange(half, dtype=np.float32) / half)).astype(np.float32)
    pos = np.arange(SEQ, dtype=np.float32)
    ang = (pos[:, None] * inv[None, :]).astype(np.float32)
    cos = np.cos(ang).astype(np.float32).T
    sin = np.sin(ang).astype(np.float32).T
    r = np.zeros((2, 128, SEQ), np.float32)
    r[0, :64] = cos
    r[0, 64:] = cos
    r[1, :64] = -sin
    r[1, 64:] = sin
    return r


WCOLS = 3080


def win_cols(hh):
    h0, h1 = 2 * hh, 2 * hh + 1
    a = np.arange(128)
    pm = (a + 64) % 128
    cols = []
    for base in (0, 512):
        cols += [base + h0 * 128 + a, base + h1 * 128 + a, base + h0 * 128 + pm, base + h1 * 128 + pm]
    cols = [cols[0], cols[1], cols[2], cols[3], cols[4], cols[5], cols[6], cols[7]]
    cols += [1536 + h0 * 128 + a, 1536 + h1 * 128 + a]
    cols += [2048 + h0 * 128 + a, 2048 + h1 * 128 + a]
    cols += [2560 + h0 * 128 + a, 2560 + h1 * 128 + a]
    cols += [3584 + hh * 256 + a, 3584 + hh * 256 + 128 + a]
    cols += [4096 + hh * 256 + a, 4096 + hh * 256 + 128 + a]
    cols += [4096 + 512 + hh * 128 + a, 4096 + 768 + hh * 128 + a]
    cols += [1024 + h0 * 128 + a, 1024 + h1 * 128 + a]
    cols += [3072 + h0 * 128 + a, 3072 + h1 * 128 + a]
    cols += [5120 + 4 * hh + np.arange(4), np.zeros(4, np.int64)]
    return np.concatenate(cols)


def mixer1_phase(S, C, l, hh, hsrc, win_d, cst_d, cstb_d, rope_d, qs_d, kt_d, v_d, y_d, yrow, ntiles=16, dbg=9):
    R = Ctx()
    S.release()
    cst = S.alloc("cst", (CL.N,))
    cstb = S.alloc("cstb", (CL.NB16,), BF16)
    S.dma("sp", lambda e: e.dma_start(out=cst[:], in_=cst_d), writes=[cst], chan="cst")
    S.dma("sp", lambda e: e.dma_start(out=cstb[:], in_=cstb_d), writes=[cstb], chan="cstb")
    ident = (cstb, cstb[:, CL.IDENT:CL.IDENT + 128])
    winb = S.alloc("winb", (8, 3200), BF16)
    S32 = [S.alloc("S32_%d" % h, (128,)) for h in range(2)]
    st32 = S.alloc("st32", (256,))
    cb = [S.alloc("cb%d" % i, (515,)) for i in range(4)]
    aneg = S.alloc("aneg", (4,))
    for h in range(2):
        S.op("pool", lambda e, h=h: e.memset(S32[h][:], 0.0), writes=[S32[h]])
    S.op("pool", lambda e: e.memset(st32[:], 0.0), writes=[st32])
    for i in range(4):
        S.op("pool", lambda e, i=i: e.memset(cb[i][:], 0.0), writes=[cb[i]])
    ACT(S, (aneg, aneg[:]), (C.pp, C.pp[:, PPL.sl("alog_%d" % l)]), AF.Exp)
    TS(S, "dve", (aneg, aneg[:]), (aneg, aneg[:]), -1.0)
    top1 = S.top
    stg = S.ring("wstg", 2, (8, 256))
    wv = win_d.rearrange("(kc p) f -> p kc f", p=128)
    for b0 in range(0, WCOLS, 256):
        w = min(256, WCOLS - b0)
        st, ch = stg.next()
        S.dma("sp", lambda e, st=st, b0=b0, w=w: e.dma_start(out=st[:, :, 0:w], in_=wv[:, :, b0:b0 + w]),
              writes=[st], chan=ch)
        CP(S, "act" if (b0 // 256) % 2 == 0 else "dve", (winb, winb[:, :, b0:b0 + w]), (st, st[:, :, 0:w]))
    S.barrier()
    S.top = top1
    R.htile = S.ring("ht", 1, (8, 512))
    R.sq = S.ring("sq", 1, (8, 512), BF16)
    R.rs = S.ring("rs", 4, (512,))
    R.tmp = S.ring("tmp", 4, (512,))
    un = S.alloc("un", (8, 512), BF16)
    roper = S.ring("rope", 2, (2, 512))
    qrot = S.ring("qrot", 2, (512,), BF16)
    qdec = S.ring("qdec", 2, (512,), BF16)
    krot = S.ring("krot", 2, (512,), BF16)
    sgr = S.ring("sgr", 2, (512,), BF16)
    b16 = S.ring("b16", 4, (512,), BF16)
    sz = [S.alloc("sz%d" % i, (512,), BF16) for i in range(2)]
    xc = [S.alloc("xc%d" % i, (512,), BF16) for i in range(4)]
    Vr = S.alloc("Vr", (4, 256), BF16)
    vs = S.ring("vs", 2, (256,), BF16)
    mret = S.alloc("mret", (512,), BF16)
    ktok = S.alloc("ktok", (2, 512), BF16)
    Sb = [S.alloc("Sb%d" % i, (128,), BF16) for i in range(8)]
    sm = S.alloc("sm", (8, 16))
    cdec = S.alloc("cdec", (8, 4))
    Rt = S.ring("Rt", 2, (2, 512), BF16)
    smb = S.alloc("smb", (2, 16), BF16)
    Et = S.ring("Et", 2, (4, 128))
    tmpd = S.ring("tmpd", 3, (4, 128))
    decT = S.ring("decT", 2, (4, 128), BF16)
    mT = [S.alloc("mT%d" % i, (4, 128), BF16) for i in range(4)]
    CTd = [S.alloc("CTd%d" % i, (4, 128), BF16) for i in range(4)]
    xdt = S.alloc("xdt", (4, 256), BF16)
    xdtd = S.alloc("xdtd", (2, 4, 256), BF16)
    dendm = S.alloc("dendm", (2, 16))
    Btok = S.alloc("Btok", (4, 128), BF16)
    stb = [S.alloc("stb%d" % i, (256,), BF16) for i in range(8)]
    tmpS = S.alloc("tmpS", (256,))
    yo = S.ring("yo", 3, (512,), BF16)
    trib = (cstb, cstb[:, CL.TRIB:CL.TRIB + 128])
    blkb = (cstb, cstb[:, CL.BLKB:CL.BLKB + 128])
    convw = C.pp[:, PPL.sl("convw_%d" % l)]
    convb = C.pp[:, PPL.sl("convb_%d" % l)]
    dsk = C.pp[:, PPL.sl("dsk_%d" % l)]
    retgn = C.pp[:, PPL.sl("retgn_%d" % l)]
    dtb = C.pp[:, PPL.sl("dtb_%d" % l)]
    sm3 = lambda i: sm[:, i, :].rearrange("p (b h) -> p b h", b=4, h=4)
    bc4 = lambda ap: ap.unsqueeze(1).to_broadcast([128, 4, 4])

    import os
    NKC = int(os.environ.get("PROJ_KC", "8"))
    PMODE = os.environ.get("PROJ_MODE", "")

    def proj_fm(ci):
        p = S.next_ps()
        if PMODE == "skip":
            return p
        if PMODE == "swap":
            S.op("pe", mm_group(S, p, p[:], [(un[:, kc, 0:128], winb[:, kc, ci * 128:ci * 128 + 512])
                                            for kc in range(NKC)]), reads=[winb, un], writes=[p])
            return p
        S.op("pe", mm_group(S, p, p[:], [(winb[:, kc, ci * 128:(ci + 1) * 128], un[:, kc, :])
                                        for kc in range(NKC)]), reads=[winb, un], writes=[p])
        return p

    for g in range(ntiles):
        t0 = g * 512
        hsrc_of = lambda gg: (hsrc(gg) if callable(hsrc) else hsrc[:, gg * 512:(gg + 1) * 512])
        if g == 0:
            ht_next = load_h_tile(S, R, hsrc_of(0))
        norm_mod_tile(S, C, R, None, un[:], un, ht=ht_next)
        if g + 1 < ntiles:
            ht_next = load_h_tile(S, R, hsrc_of(g + 1))
        ydg, yc0 = (y_d(g) if callable(y_d) else (y_d, t0))

        def load_rope(gg):
            rt, chr_ = roper.next()
            S.dma("sp", lambda e, rt=rt, gg=gg: e.dma_start(
                out=rt[:], in_=rope_d[:, :, gg * 512:(gg + 1) * 512].rearrange("a p t -> p a t")),
                writes=[rt], chan=chr_)
            return rt
        if g == 0:
            rope_next = load_rope(0)
        ropeT = rope_next
        if g + 1 < ntiles:
            rope_next = load_rope(g + 1)
        cosT = (ropeT, ropeT[:, 0, :])
        sinT = (ropeT, ropeT[:, 1, :])

        def rope(ci, cip, outb):
            p1 = proj_fm(ci)
            p2 = proj_fm(cip)
            t1, _ = R.tmp.next()
            t2, _ = R.tmp.next()
            if dbg <= 2.0001:
                S.free_ps(p1)
                S.free_ps(p2)
                return
            if dbg <= 2.0002:
                CP(S, "dve", (t1, t1[:]), (p1, p1[:]))
                CP(S, "dve", (t2, t2[:]), (p2, p2[:]))
                S.free_ps(p1)
                S.free_ps(p2)
                return
            TT(S, "dve", (t1, t1[:]), (p1, p1[:]), cosT, ALU.mult)
            TT(S, "dve", (t2, t2[:]), (p2, p2[:]), sinT, ALU.mult)
            S.free_ps(p1)
            S.free_ps(p2)
            if dbg <= 2.001:
                return
            TT(S, "pool", (outb, outb[:]), (t1, t1[:]), (t2, t2[:]), ALU.add)

        pdt = S.next_ps()
        for blk in range(4):
            ptm = S.next_ps()
            S.op("pe", mm_group(S, ptm, ptm[:], [(un[:, kc, blk * 128:(blk + 1) * 128], winb[:, kc, 2560:3072])
                                                for kc in range(8)]), reads=[winb, un], writes=[ptm])
            CP(S, "act", (Vr, Vr[:, blk, :]), (ptm, ptm[:, 0:256]))
            v1, chv = vs.next()
            CP(S, "dve", (v1, v1[:]), (ptm, ptm[:, 256:512]))
            S.free_ps(ptm)
            S.dma("sp", lambda e, v1=v1, blk=blk, t0=t0: e.dma_start(
                out=v_d[t0 + blk * 128:t0 + (blk + 1) * 128, :], in_=v1[:]), reads=[v1], chan=chv)
            S.op("pe", mm_group(S, pdt, pdt[:, blk * 128:(blk + 1) * 128],
                                [(un[:, kc, blk * 128:(blk + 1) * 128], winb[:, kc, 2948:3076]) for kc in range(8)]),
                 reads=[winb, un], writes=[pdt])
        TT(S, "dve", (sm, sm3(0)), (pdt, pdt[:].rearrange("p (b c) -> p b c", b=4, c=128)[:, :, 124:128]),
           (C.pp, bc4(dtb)), ALU.add)
        S.free_ps(pdt)
        ACT(S, (sm, sm[:, 1, :]), (sm, sm[:, 0, :]), AF.Exp)
        ACT(S, (sm, sm[:, 2, :]), (sm, sm[:, 1, :]), AF.Ln, bias=(C.onec, C.onec[:]))
        TT(S, "dve", (sm, sm3(3)), (sm, sm3(2)), (aneg, bc4(aneg[:])), ALU.mult)
        CP(S, "dve", (smb, smb[:, 0, :]), (sm, sm[:, 3, :]))
        TT(S, "dve", (smb, smb[:, 1, :]), (sm, sm[:, 3, :]), (smb, smb[:, 0, :]), ALU.subtract)
        if dbg <= 1:
            continue
        for h in range(2):
            qr, _ = qrot.next()
            rope(0 + h, 2 + h, qr)
            if dbg <= 2.002:
                continue
            qd, _ = qdec.next()
            TT(S, "pool", (qd, qd[:]), (qr, qr[:]), (cst, cst[:, CL.QDEC + h * 512:CL.QDEC + (h + 1) * 512]), ALU.mult)
            if dbg <= 2.01:
                continue
            kr, _ = krot.next()
            rope(4 + h, 6 + h, kr)
            if dbg <= 2.02:
                continue
            pg = proj_fm(8 + h)
            sg, _ = sgr.next()
            ACT(S, (sg, sg[:]), (pg, pg[:]), AF.Silu)
            S.free_ps(pg)
            if dbg <= 2.03:
                continue
            pq = proj_fm(10 + h)
            qsb, chq = b16.next()
            ACT(S, (qsb, qsb[:]), (pq, pq[:]), AF.Copy, scale=128.0 ** -0.5)
            S.free_ps(pq)
            S.dma("sp", lambda e, qsb=qsb, h=h, t0=t0: e.dma_start(
                out=qs_d[h * 128:(h + 1) * 128, t0:t0 + 512], in_=qsb[:]), reads=[qsb], chan=chq)
            if dbg <= 2.04:
                continue
            pk = proj_fm(12 + h)
            ksb, chk = b16.next()
            CP(S, "dve", (ksb, ksb[:]), (pk, pk[:]))
            S.free_ps(pk)
            S.dma("sp", lambda e, ksb=ksb, h=h, t0=t0: e.dma_start(
                out=kt_d[h * 128:(h + 1) * 128, t0:t0 + 512], in_=ksb[:]), reads=[ksb], chan=chk)
            if dbg <= 2.1:
                continue
            psc = S.next_ps()

            def fn_sc(e, kr=kr, qr=qr, psc=psc):
                r = None
                for blk in range(4):
                    sl = slice(blk * 128, (blk + 1) * 128)
                    r = e.matmul(psc[:, sl], kr[:, sl], qr[:, sl], start=True, stop=True)
                return r
            S.op("pe", fn_sc, reads=[kr, qr], writes=[psc])
            TT(S, "dve", (mret, mret[:]), (psc, psc[:]), (cst, cst[:, CL.DMAT + h * 512:CL.DMAT + (h + 1) * 512]), ALU.mult)
            S.free_ps(psc)
            if dbg <= 2.2:
                continue
            ptk = S.next_ps()
            ptkb = ptk[:].bitcast(BF16)

            def fn_tk(e, kr=kr, ptkb=ptkb):
                r = None
                for blk in range(4):
                    sl = slice(blk * 128, (blk + 1) * 128)
                    r = e.transpose(ptkb[:, sl], kr[:, sl], ident[1])
                return r
            S.op("pe", fn_tk, reads=[kr, cstb], writes=[ptk])
            for half in range(2):
                kc_ = CL.KDEC + h * 2 + half
                TS(S, "dve", (ktok, ktok[:, half, :]), (ptk, ptkb[:, 0:512]), cst[:, kc_:kc_ + 1], sb=[cst])
            S.free_ps(ptk)
            if dbg <= 2.3:
                continue
            pkv = [S.next_ps(), S.next_ps()]

            def fn_kv(e, h=h, pkv=pkv):
                r = None
                for c in range(8):
                    blk, half = c // 2, c % 2
                    r = e.matmul(pkv[c // 4][:, (c % 4) * 128:(c % 4 + 1) * 128],
                                 ktok[:, half, blk * 128:(blk + 1) * 128], Vr[:, blk, h * 128:(h + 1) * 128],
                                 start=True, stop=True)
                return r
            S.op("pe", fn_kv, reads=[ktok, Vr], writes=pkv)
            for c in range(8):
                CP(S, "act", (Sb[c], Sb[c][:]), (S32[h], S32[h][:]))
                pk_ = pkv[c // 4]
                STT(S, (S32[h], S32[h][:]), (S32[h], S32[h][:]), cst[:, CL.GL + h:CL.GL + h + 1],
                    (pk_, pk_[:, (c % 4) * 128:(c % 4 + 1) * 128]), ALU.mult, ALU.add, sb=[cst])
            S.free_ps(pkv[0])
            S.free_ps(pkv[1])
            if dbg <= 2.4:
                continue
            py = S.next_ps()

            def fn_y(e, h=h, py=py, qd=qd):
                r = None
                for blk in range(4):
                    sl = slice(blk * 128, (blk + 1) * 128)
                    e.matmul(py[:, sl], Vr[:, blk, h * 128:(h + 1) * 128], mret[:, sl], start=True, stop=False)
                    for half in range(2):
                        s2 = slice(blk * 128 + half * 64, blk * 128 + (half + 1) * 64)
                        r = e.matmul(py[:, s2], Sb[blk * 2 + half][:], qd[:, s2], start=False, stop=(half == 1))
                return r
            S.op("pe", fn_y, reads=[Vr, mret, qd] + Sb, writes=[py])
            ysq, _ = b16.next()
            ACT(S, (ysq, ysq[:]), (py, py[:]), AF.Square)
            pss = S.next_ps()
            S.op("pe", lambda e, pss=pss, ysq=ysq: e.matmul(pss[:], C.ones_bf[:], ysq[:], start=True, stop=True),
                 reads=[C.ones_bf, ysq], writes=[pss])
            lnv, _ = R.rs.next()
            ACT(S, (lnv, lnv[:]), (pss, pss[:]), AF.Ln, bias=(C.epsc, C.epsc[:]), scale=1.0 / 128)
            S.free_ps(pss)
            rstd, _ = R.rs.next()
            ACT(S, (rstd, rstd[:]), (lnv, lnv[:]), AF.Exp, scale=-0.5)
            t1, _ = R.tmp.next()
            STT(S, (t1, t1[:]), (py, py[:]), retgn[:, h:h + 1], (rstd, rstd[:]), ALU.mult, ALU.mult, sb=[C.pp])
            S.free_ps(py)
            yb, chy = yo.next()
            TT(S, "pool", (yb, yb[:]), (t1, t1[:]), (sg, sg[:]), ALU.mult)
            r0 = yrow["ret%d" % h]
            S.dma("sp", lambda e, yb=yb, r0=r0, ydg=ydg, yc0=yc0: e.dma_start(out=ydg[r0:r0 + 128, yc0:yc0 + 512], in_=yb[:]),
                  reads=[yb], chan=chy)
        if dbg < 3:
            continue
        for c in range(2):
            pz = proj_fm(14 + c)
            ACT(S, (sz[c], sz[c][:]), (pz, pz[:]), AF.Silu)
            S.free_ps(pz)
        for ch in range(4):
            pc = proj_fm(16 + ch)
            S.op("pool", lambda e, ch=ch: e.tensor_copy(out=cb[ch][:, 0:3], in_=cb[ch][:, 512:515]),
                 reads=[cb[ch]], writes=[cb[ch]])
            CP(S, "act", (cb[ch], cb[ch][:, 3:515]), (pc, pc[:]))
            S.free_ps(pc)
            acc, _ = R.tmp.next()
            TS(S, "dve", (acc, acc[:]), (cb[ch], cb[ch][:, 0:512]), convw[:, ch * 4:ch * 4 + 1], sb=[C.pp])
            for j in range(1, 4):
                STT(S, (acc, acc[:]), (cb[ch], cb[ch][:, j:j + 512]), convw[:, ch * 4 + j:ch * 4 + j + 1],
                    (acc, acc[:]), ALU.mult, ALU.add, sb=[C.pp])
            ACT(S, (xc[ch], xc[ch][:]), (acc, acc[:]), AF.Silu, bias=(C.pp, convb[:, ch:ch + 1]))
        BT, CT = xc[2], xc[3]
        pcb = S.next_ps()

        def fn_cb(e, pcb=pcb):
            r = None
            for blk in range(4):
                sl = slice(blk * 128, (blk + 1) * 128)
                r = e.matmul(pcb[:, sl], BT[:, sl], CT[:, sl], start=True, stop=True)
            return r
        S.op("pe", fn_cb, reads=[BT, CT], writes=[pcb])
        ptx = S.next_ps()
        ptxb = ptx[:].bitcast(BF16)

        def fn_tx(e, ptxb=ptxb):
            r = None
            for blk in range(4):
                for c in range(2):
                    r = e.transpose(ptxb[:, blk * 256 + c * 128:blk * 256 + (c + 1) * 128],
                                    xc[c][:, blk * 128:(blk + 1) * 128], ident[1])
            return r
        S.op("pe", fn_tx, reads=[xc[0], xc[1], cstb], writes=[ptx])
        dt16 = sm[:, 2, :].unsqueeze(2).to_broadcast([128, 16, 64])
        v16 = lambda ap: ap.rearrange("p (a b) -> p a b", a=16, b=64)
        TT(S, "dve", (xdt, v16(xdt[:].rearrange("p a b -> p (a b)"))), (ptx, v16(ptxb[:, 0:1024])), (sm, dt16), ALU.mult)
        S.free_ps(ptx)
        ptB = S.next_ps()
        ptBb = ptB[:].bitcast(BF16)

        def fn_tb(e, ptBb=ptBb):
            r = None
            for blk in range(4):
                sl = slice(blk * 128, (blk + 1) * 128)
                r = e.transpose(ptBb[:, sl], BT[:, sl], ident[1])
            return r
        S.op("pe", fn_tb, reads=[BT, cstb], writes=[ptB])
        CP(S, "act", (Btok, Btok[:].rearrange("p a b -> p (a b)")), (ptB, ptBb[:, 0:512]))
        S.free_ps(ptB)
        for blk in range(4):
            Rb, _ = Rt.next()
            for hl in range(2):
                TT(S, "dve", (Rb, Rb[:, hl, :].rearrange("p (a b) -> p a b", a=4, b=128)),
                   (cstb, trib[1].unsqueeze(1).to_broadcast([128, 4, 128])),
                   (smb, smb[:, hl, blk * 4:(blk + 1) * 4].unsqueeze(2).to_broadcast([128, 4, 128])), ALU.mult)
            pab = S.next_ps()

            def fn_ab(e, pab=pab, Rb=Rb):
                e.matmul(pab[:], C.ones_bf[:], Rb[:, 0, :], start=True, stop=False)
                return e.matmul(pab[:], C.ones_bf[:], Rb[:, 1, :], start=False, stop=True)
            S.op("pe", fn_ab, reads=[C.ones_bf, Rb], writes=[pab])
            Eb, _ = Et.next()
            ACT(S, (Eb, Eb[:].rearrange("p a b -> p (a b)")), (pab, pab[:]), AF.Exp)
            for half in range(2):
                cc = blk * 2 + half
                CP(S, "pool", (cdec, cdec[:, cc, :]), (Eb, Eb[:, :, 63 + 64 * half]))
            tq, _ = tmpd.next()
            pab3 = pab[:].rearrange("p (a b) -> p a b", a=4, b=128)
            TT(S, "dve", (tq, tq[:]), (pab, pab3), (cstb, ident[1].unsqueeze(1).to_broadcast([128, 4, 128])), ALU.mult)
            S.op("dve", lambda e, tq=tq, blk=blk: e.tensor_reduce(
                out=sm[:, 4, blk * 4:(blk + 1) * 4], in_=tq[:], axis=mybir.AxisListType.X, op=ALU.add),
                reads=[tq], writes=[sm])
            TS(S, "dve", (sm, sm[:, 5, blk * 4:(blk + 1) * 4]), (sm, sm[:, 4, blk * 4:(blk + 1) * 4]), -1.0)
            for half in range(2):
                rows = slice(half * 64, (half + 1) * 64)
                CP(S, "act", (sm, sm[rows, 6, blk * 4:(blk + 1) * 4]), (pab, pab3[rows, :, 63 + 64 * half]))
            td, _ = tmpd.next()
            TT(S, "dve", (td, td[:]), (pab, pab[:].rearrange("p (a b) -> p a b", a=4, b=128)),
               (cst, cst[:, CL.NEGM:CL.NEGM + 128].unsqueeze(1).to_broadcast([128, 4, 128])), ALU.add)
            S.free_ps(pab)
            dT, _ = decT.next()
            for h4 in range(4):
                ACT(S, (dT, dT[:, h4, :]), (td, td[:, h4, :]), AF.Exp, bias=(sm, sm[:, 5, blk * 4 + h4:blk * 4 + h4 + 1]))
            TT(S, "dve", (mT[blk], mT[blk][:]), (dT, dT[:]),
               (pcb, pcb[:, blk * 128:(blk + 1) * 128].unsqueeze(1).to_broadcast([128, 4, 128])), ALU.mult)
            TT(S, "pool", (CTd[blk], CTd[blk][:]), (Eb, Eb[:]),
               (CT, CT[:, blk * 128:(blk + 1) * 128].unsqueeze(1).to_broadcast([128, 4, 128])), ALU.mult)
        S.free_ps(pcb)
        TT(S, "dve", (sm, sm[:, 6, :]), (sm, sm[:, 6, :]), (sm, sm[:, 4, :]), ALU.subtract)
        ACT(S, (sm, sm[:, 7, :]), (sm, sm[:, 6, :]), AF.Exp)
        for half in range(2):
            TS(S, "dve", (dendm, dendm[:, half, :]), (sm, sm[:, 7, :]), cst[:, CL.HMASK + half:CL.HMASK + half + 1], sb=[cst])
            TT(S, "pool", (xdtd, v16(xdtd[:, half, :, :].rearrange("p a b -> p (a b)"))),
               (xdt, v16(xdt[:].rearrange("p a b -> p (a b)"))),
               (dendm, dendm[:, half, :].unsqueeze(2).to_broadcast([128, 16, 64])), ALU.mult)
        for hf in range(2):
            pst = [S.next_ps(), S.next_ps()]

            def fn_st(e, hf=hf, pst=pst):
                r = None
                for k in range(4):
                    c = hf * 4 + k
                    blk, half = c // 2, c % 2
                    r = e.matmul(pst[k // 2][:, (k % 2) * 256:(k % 2 + 1) * 256], Btok[:, blk, :],
                                 xdtd[:, half, blk, :], start=True, stop=True)
                return r
            S.op("pe", fn_st, reads=[Btok, xdtd], writes=pst)
            for k in range(4):
                c = hf * 4 + k
                CP(S, "act", (stb[c], stb[c][:]), (st32, st32[:]))
                TT(S, "dve", (tmpS, tmpS[:].rearrange("p (a b) -> p a b", a=4, b=64)),
                   (st32, st32[:].rearrange("p (a b) -> p a b", a=4, b=64)),
                   (cdec, cdec[:, c, :].unsqueeze(2).to_broadcast([128, 4, 64])), ALU.mult)
                TT(S, "dve", (st32, st32[:]), (tmpS, tmpS[:]),
                   (pst[k // 2], pst[k // 2][:, (k % 2) * 256:(k % 2 + 1) * 256]), ALU.add)
            S.free_ps(pst[0])
            S.free_ps(pst[1])
        for c in range(2):
            yb, chy = yo.next()
            for hp in range(2):
                h4 = c * 2 + hp
                pyh = S.next_ps()

                def fn_yh(e, h4=h4, c=c, pyh=pyh):
                    r = None
                    for blk in range(4):
                        sl = slice(blk * 128, (blk + 1) * 128)
                        e.matmul(pyh[:, sl], xdt[:, blk, c * 128:(c + 1) * 128], mT[blk][:, h4, :], start=True, stop=False)
                        for half in range(2):
                            s2 = slice(blk * 128 + half * 64, blk * 128 + (half + 1) * 64)
                            r = e.matmul(pyh[:, s2], stb[blk * 2 + half][:, c * 128:(c + 1) * 128],
                                         CTd[blk][:, h4, half * 64:(half + 1) * 64], start=False, stop=(half == 1))
                    return r
                S.op("pe", fn_yh, reads=[xdt] + mT + stb + CTd, writes=[pyh])
                rows = slice(hp * 64, (hp + 1) * 64)
                t1, _ = R.tmp.next()
                STT(S, (t1, t1[rows, :]), (xc[c], xc[c][rows, :]), dsk[rows, c:c + 1], (pyh, pyh[rows, :]),
                    ALU.mult, ALU.add, sb=[C.pp])
                S.free_ps(pyh)
                TT(S, "pool", (yb, yb[rows, :]), (t1, t1[rows, :]), (sz[c], sz[c][rows, :]), ALU.mult)
            r0 = yrow["ssm%d" % c]
            S.dma("sp", lambda e, yb=yb, r0=r0, ydg=ydg, yc0=yc0: e.dma_start(out=ydg[r0:r0 + 128, yc0:yc0 + 512], in_=yb[:]),
                  reads=[yb], chan=chy)
    S.barrier()


def mixer2_phase(S, C, qs_d, kt_d, v_d, cst_d, cstb_d, y_d, yrow, ntiles=16):
    S.release()
    cst = S.alloc("cst2", (CL.N,))
    cstb = S.alloc("cstb2", (CL.NB16,), BF16)
    S.dma("sp", lambda e: e.dma_start(out=cst[:], in_=cst_d), writes=[cst], chan="cst")
    S.dma("sp", lambda e: e.dma_start(out=cstb[:], in_=cstb_d), writes=[cstb], chan="cstb")
    nk = ntiles * 512
    KT = S.alloc("KT", (2, nk), BF16)
    Vt = S.alloc("Vt", (ntiles * 4, 256), BF16)
    S.dma("sp", lambda e: e.dma_start(out=KT[:], in_=kt_d[:, 0:nk].rearrange("(h p) t -> p h t", p=128)),
          writes=[KT], chan="ktload")
    S.dma("sp", lambda e: e.dma_start(out=Vt[:], in_=v_d[0:nk, :].rearrange("(b p) c -> p b c", p=128)),
          writes=[Vt], chan="vload")
    Ur = S.alloc("Ur", (128,), BF16)
    onesr = C.ones_bf
    CP(S, "act", (Ur, Ur[:]), (cst, cst[:, CL.U:CL.U + 128]))
    qsr = S.ring("qs", 2, (2, 512), BF16)
    er = S.ring("e", 2, (512,))
    spr = S.ring("sp", 6, (512,))
    lkr = S.ring("lk", 4, (512,), BF16)
    Lsr = S.ring("Lsum", 4, (512,), BF16)
    argr = S.ring("arg", 4, (512,))
    wr = S.ring("w", 4, (512,), BF16)
    yo = S.ring("yo2", 2, (512,), BF16)
    f32v = lambda b: b[:]
    DEPTH = 3
    for g in range(ntiles):
        t0 = g * 512
        qs, chq = qsr.next()
        S.dma("sp", lambda e, qs=qs, t0=t0: e.dma_start(
            out=qs[:], in_=qs_d[:, t0:t0 + 512].rearrange("(h p) t -> p h t", p=128)), writes=[qs], chan=chq)
        for h in range(2):
            pyT = S.next_ps()
            nkb = 4 * g + 4
            kbs = list(range(nkb - 1, -1, -1))
            st = {}
            st2 = {}
            lsc = {"cur": None}

            def s0(idx, h=h, qs=qs, kbs=kbs, st=st):
                kb = kbs[idx]
                pz = S.next_ps()
                S.op("pe", lambda e, pz=pz, kb=kb: e.matmul(
                    pz[:], KT[:, h, kb * 128:(kb + 1) * 128], qs[:, h, :], start=True, stop=True),
                    reads=[KT, qs], writes=[pz])
                st[idx] = {"pz": pz}

            def s1(idx, st=st):
                d = st[idx]
                ee, _ = er.next()
                ACT(S, (ee, ee[:]), (d["pz"], d["pz"][:]), AF.Exp, scale=-1.0)
                sp, _ = spr.next()
                ACT(S, (sp, sp[:]), (ee, ee[:]), AF.Ln, bias=(C.onec, C.onec[:]))
                d["sp"] = sp

            def s2(idx, g=g, kbs=kbs, st=st):
                d = st[idx]
                ing = kbs[idx] - 4 * g
                lk, _ = lkr.next()
                STT(S, (lk, lk[:]), (d["sp"], d["sp"][:]), -1.0, (d["pz"], d["pz"][:]), ALU.mult, ALU.subtract)
                S.free_ps(d["pz"])
                d["msk"] = None
                if ing >= 0:
                    d["msk"] = (cstb, cstb[:, CL.SBM + ing * 512:CL.SBM + (ing + 1) * 512])
                    TT(S, "pool", (lk, lk[:]), (lk, lk[:]), d["msk"], ALU.mult)
                d["lk"] = lk

            def s3(idx, kbs=kbs, st=st, lsc=lsc):
                d = st[idx]
                first = idx == 0
                last = kbs[idx] == 0
                lk = d["lk"]
                pt = S.next_ps()
                Ls = lsc["cur"]

                def fn_t(e, pt=pt, lk=lk, first=first, Ls=Ls):
                    r = e.matmul(pt[:], Ur[:], lk[:], start=True, stop=first)
                    if not first:
                        r = e.matmul(pt[:], onesr[:], Ls[:], start=False, stop=True)
                    return r
                S.op("pe", fn_t, reads=[Ur, onesr, lk] + ([] if first else [Ls]), writes=[pt])
                if not last:
                    Ln_, _ = Lsr.next()
                    if first:
                        CP(S, "pool", (Ln_, Ln_[:]), (lk, lk[:]))
                    else:
                        TT(S, "pool", (Ln_, Ln_[:]), (Ls, Ls[:]), (lk, lk[:]), ALU.add)
                    lsc["cur"] = Ln_
                d["pt"] = pt

            def s4(idx, st=st):
                d = st[idx]
                ag, _ = argr.next()
                TT(S, "dve", (ag, ag[:]), (d["pt"], d["pt"][:]), (d["sp"], d["sp"][:]), ALU.subtract)
                S.free_ps(d["pt"])
                d["ag"] = ag

            def s5(idx, st=st):
                d = st[idx]
                w, _ = wr.next()
                ACT(S, (w, w[:]), (d["ag"], d["ag"][:]), AF.Exp)
                if d["msk"] is not None:
                    TT(S, "pool", (w, w[:]), (w, w[:]), d["msk"], ALU.mult)
                d["w"] = w

            def s6(idx, h=h, kbs=kbs, st=st, pyT=pyT):
                d = st.pop(idx)
                kb = kbs[idx]
                first = idx == 0
                last = kb == 0
                w = d["w"]
                S.op("pe", lambda e, w=w, kb=kb, first=first, last=last: e.matmul(
                    pyT[:], Vt[:, kb, h * 128:(h + 1) * 128], w[:], start=first, stop=last),
                    reads=[Vt, w], writes=[pyT])

            n = len(kbs)
            stages = [s0, s1, s2, s3, s4, s5, s6]
            for step in range(n + len(stages) - 1):
                for si, fn_s in enumerate(stages):
                    i = step - si
                    if 0 <= i < n:
                        fn_s(i)
            yb, chy = yo.next()
            CP(S, "act", (yb, yb[:]), (pyT, pyT[:]))
            S.free_ps(pyT)
            r0 = yrow["sb%d" % h]
            ydg, yc0 = (y_d(g) if callable(y_d) else (y_d, t0))
            S.dma("sp", lambda e, yb=yb, r0=r0, ydg=ydg, yc0=yc0: e.dma_start(out=ydg[r0:r0 + 128, yc0:yc0 + 512], in_=yb[:]),
                  reads=[yb], chan=chy)
    S.barrier()


def wout_phase(S, C, l, y_d, ycol0, wout_d, hsrc, hdst, ntok, y2_d=None, ssm_pos=(8, 9, 10, 11)):
    S.release()
    wob = S.alloc("wob", (12, D), BF16)
    stg = S.ring("wostg", 2, (12, 256))
    wv = wout_d.rearrange("(fc p) d -> p fc d", p=128)
    for b0 in range(0, D, 256):
        st, ch = stg.next()
        S.dma("sp", lambda e, st=st, b0=b0: e.dma_start(out=st[:], in_=wv[:, :, b0:b0 + 256]), writes=[st], chan=ch)
        CP(S, "act" if (b0 // 256) % 2 == 0 else "dve", (wob, wob[:, :, b0:b0 + 256]), (st, st[:]))
    ytr = S.ring("yt", 2, (12, 512), BF16)
    yt2r = S.ring("yt2", 2, (12, 512), BF16)
    sqr = S.ring("ysq", 1, (4, 512), BF16)
    rsr = S.ring("rs", 2, (512,))
    tmpr = S.ring("tmp", 2, (512,))
    resr = S.ring("res", 3, (512,))
    outr = S.ring("outt", 3, (512,))
    ssmn = C.pp[:, PPL.sl("ssmn_%d" % l)]
    for tt in range(ntok // 512):
        t0 = tt * 512
        yt, chy = ytr.next()
        S.dma("sp", lambda e, yt=yt, t0=t0, tt=tt: e.dma_start(
            out=yt[:], in_=(y_d(tt) if callable(y_d) else y_d[:, ycol0 + t0:ycol0 + t0 + 512]).rearrange("(c p) t -> p c t", p=128)),
            writes=[yt], chan=chy)
        if y2_d is not None:
            yt2, chy2 = yt2r.next()
            S.dma("sp", lambda e, yt2=yt2, t0=t0, tt=tt: e.dma_start(
                out=yt2[:], in_=(y2_d(tt) if callable(y2_d) else y2_d[:, ycol0 + t0:ycol0 + t0 + 512]).rearrange("(c p) t -> p c t", p=128)),
                writes=[yt2], chan=chy2)
            ys = C.pp[:, PPL.sl("ysel")]
            TS(S, "dve", (yt, yt[:]), (yt, yt[:]), ys[:, 0:1], sb=[C.pp])
            STT(S, (yt, yt[:]), (yt2, yt2[:]), ys[:, 1:2], (yt, yt[:]), ALU.mult, ALU.add, sb=[C.pp])
        sq, _ = sqr.next()
        for c in range(4):
            ACT(S, (sq, sq[:, c, :]), (yt, yt[:, ssm_pos[c], :]), AF.Square)
        pss = S.next_ps()

        def fn(e, pss=pss, sq=sq):
            r = None
            for c in range(4):
                r = e.matmul(pss[:], C.ones_bf[:], sq[:, c, :], start=(c == 0), stop=(c == 3))
            return r
        S.op("pe", fn, reads=[sq, C.ones_bf], writes=[pss])
        lnv, _ = rsr.next()
        ACT(S, (lnv, lnv[:]), (pss, pss[:]), AF.Ln, bias=(C.epsc, C.epsc[:]), scale=1.0 / 512)
        S.free_ps(pss)
        rstd, _ = rsr.next()
        ACT(S, (rstd, rstd[:]), (lnv, lnv[:]), AF.Exp, scale=-0.5)
        for c in range(4):
            STT(S, (yt, yt[:, ssm_pos[c], :]), (yt, yt[:, ssm_pos[c], :]), ssmn[:, c:c + 1], (rstd, rstd[:]),
                ALU.mult, ALU.mult, sb=[C.pp])
        for dc in range(8):
            res, ch = resr.next()
            S.dma("sp", lambda e, res=res, dc=dc, t0=t0: e.dma_start(
                out=res[:], in_=hsrc[dc * 128:(dc + 1) * 128, t0:t0 + 512]), writes=[res], chan=ch)
            po = S.next_ps()
            S.op("pe", mm_group(S, po, po[:], [(wob[:, fc, dc * 128:(dc + 1) * 128], yt[:, fc, :]) for fc in range(12)]),
                 reads=[wob, yt], writes=[po])
            ot, ch2 = outr.next()
            STT(S, (ot, ot[:]), (po, po[:]), C.Gsc[:, dc:dc + 1], (res, res[:]), ALU.mult, ALU.add, sb=[C.Gsc])
            S.free_ps(po)
            S.dma("pool", lambda e, ot=ot, dc=dc, t0=t0: e.dma_start(
                out=hdst[dc * 128:(dc + 1) * 128, t0:t0 + 512], in_=ot[:]), reads=[ot], chan=ch2)
    S.barrier()


def final_phase(S, C, hsrc, odst, ntok):
    S.release()
    R = Ctx()
    R.htile = S.ring("ht", 2, (8, 512))
    R.sq = S.ring("sq", 1, (8, 512), BF16)
    R.rs = S.ring("rs", 4, (512,))
    R.tmp = S.ring("tmp", 3, (512,))
    outr = S.ring("fo", 2, (8, 512))
    for tt in range(ntok // 512):
        t0 = tt * 512
        ot, ch = outr.next()
        norm_mod_tile(S, C, R, hsrc[:, t0:t0 + 512], ot[:], ot)
        S.dma("pool", lambda e, ot=ot, t0=t0: e.dma_start(
            out=odst[:, t0:t0 + 512].rearrange("(c p) t -> p c t", p=128), in_=ot[:]), reads=[ot], chan=ch)
    S.barrier()


def mod_setup_norm(S, C, adaw_d, j0, adab_ap, gain_ap, nj=2):
    compute_mod(S, C, adaw_d, 0, j0, nj, adab_ap)
    S.op("dve", lambda e: e.scalar_tensor_tensor(out=C.Asc[:], in0=C.modv[:, 8:16], scalar=1.0,
                                                 in1=gain_ap, op0=ALU.add, op1=ALU.mult),
         reads=[C.modv, C.pp], writes=[C.Asc])


def mod_setup_gate(S, C, adaw_d, j, adab_ap, half):
    S.release()
    mstage = S.ring("mstg", 3, (2048,))
    pm = S.next_ps()
    blocked = len(adaw_d.shape) == 3
    wv = None if blocked else adaw_d.rearrange("(kc p) f -> p kc f", p=128)
    for blk in range(4):
        col0 = j * 1024 + blk * 256
        st, ch = mstage.next()
        sv = st[:, 0:2048].rearrange("p (kc f) -> p kc f", kc=8, f=256)
        src = (adaw_d[col0 // 256].rearrange("p (kc f) -> p kc f", kc=8, f=256) if blocked
               else wv[:, :, col0:col0 + 256])
        S.dma("sp", lambda e, sv=sv, src=src: e.dma_start(out=sv, in_=src),
              writes=[st], chan=ch)
        for cc in range(2):
            oc = blk * 2 + cc

            def fn(e, sv=sv, cc=cc, oc=oc):
                r = None
                for kc in range(8):
                    r = e.matmul(pm[:, oc:oc + 1], sv[:, kc, cc * 128:(cc + 1) * 128],
                                 C.cond[:, kc:kc + 1], start=(kc == 0), stop=(kc == 7))
                return r
            S.op("pe", fn, reads=[st, C.cond], writes=[pm])
    S.op("dve", lambda e: e.tensor_tensor(out=C.modv[:, 16:24], in0=pm[:, 0:8], in1=adab_ap, op=ALU.add),
         reads=[pm, C.pp], writes=[C.modv])
    S.free_ps(pm)
    S.op("dve", lambda e: e.tensor_scalar(out=C.Gsc[:], in0=C.modv[:, 16:24], scalar1=1.0,
                                          scalar2=(0.5 if half else 1.0), op0=ALU.add, op1=ALU.mult),
         reads=[C.modv], writes=[C.Gsc])
    S.barrier()


YROW_OWN = {"ret0": 0, "ret1": 128, "sb0": 256, "sb1": 384, "ssm0": 512, "ssm1": 640}


def build_B(hh, ntiles=16, dbg=9):
    nc = bass.Bass("TRN2", target_bir_lowering=False)
    ns = ntiles * 512
    hfull = nc.dram_tensor("hfull", [D, ns], F32, kind="ExternalInput").ap()
    pp_d = nc.dram_tensor("pp", [128, PPL.n], F32, kind="ExternalInput").ap()
    adaw = nc.dram_tensor("adaw", [D, 9 * D], F32, kind="ExternalInput").ap()
    win = nc.dram_tensor("win", [D, WCOLS], F32, kind="ExternalInput").ap()
    cst_d = nc.dram_tensor("cst", [128, CL.N], F32, kind="ExternalInput").ap()
    cstb_d = nc.dram_tensor("cstb", [128, CL.NB16], BF16, kind="ExternalInput").ap()
    rope_d = nc.dram_tensor("rope", [2, 128, SEQ], F32, kind="ExternalInput").ap()
    yown = nc.dram_tensor("yown", [768, ns], BF16, kind="ExternalOutput").ap()
    qs_d = nc.dram_tensor("qs_d", [256, ns], BF16).ap()
    kt_d = nc.dram_tensor("kt_d", [256, ns], BF16).ap()
    v_d = nc.dram_tensor("v_d", [ns, 256], BF16).ap()
    with contextlib.ExitStack() as es:
        S = Sched(nc, es)
        C = Ctx()
        setup_common(S, C, pp_d)
        mod_setup_norm(S, C, adaw, 3, C.pp[:, PPL.sl("adab_0", 24, 40)], C.pp[:, PPL.sl("g_mix_0")])
        mixer1_phase(S, C, 0, hh, hfull, win, cst_d, cstb_d, rope_d, qs_d, kt_d, v_d, yown, YROW_OWN, ntiles, dbg)
        if dbg >= 4:
            mixer2_phase(S, C, qs_d, kt_d, v_d, cst_d, cstb_d, yown, YROW_OWN, ntiles)
        S.emit()
    return nc


def build_C(last, ntok=NTOK):
    nc = bass.Bass("TRN2", target_bir_lowering=False)
    hin = nc.dram_tensor("hin", [D, ntok], F32, kind="ExternalInput").ap()
    y = nc.dram_tensor("y", [1536, ntok], BF16, kind="ExternalInput").ap()
    pp_d = nc.dram_tensor("pp", [128, PPL.n], F32, kind="ExternalInput").ap()
    adaw = nc.dram_tensor("adaw", [D, 9 * D], F32, kind="ExternalInput").ap()
    wout = nc.dram_tensor("wout", [1536, D], F32, kind="ExternalInput").ap()
    wg = nc.dram_tensor("wg", [D, DFF], F32, kind="ExternalInput").ap()
    wu = nc.dram_tensor("wu", [D, DFF], F32, kind="ExternalInput").ap()
    wd = nc.dram_tensor("wd", [DFF, D], F32, kind="ExternalInput").ap()
    if last:
        adaw2 = nc.dram_tensor("adaw2", [D, 2 * D], F32, kind="ExternalInput").ap()
    else:
        adaw2 = nc.dram_tensor("adaw2", [D, 9 * D], F32, kind="ExternalInput").ap()
        wg2 = nc.dram_tensor("wg2", [D, DFF], F32, kind="ExternalInput").ap()
        wu2 = nc.dram_tensor("wu2", [D, DFF], F32, kind="ExternalInput").ap()
        wd2 = nc.dram_tensor("wd2", [DFF, D], F32, kind="ExternalInput").ap()
    hout = nc.dram_tensor("hout", [D, ntok], F32, kind="ExternalOutput").ap()
    hb = nc.dram_tensor("hb", [D, ntok], F32).ap()
    with contextlib.ExitStack() as es:
        S = Sched(nc, es)
        C = Ctx()
        setup_common(S, C, pp_d)
        mod_setup_gate(S, C, adaw, 5, C.pp[:, PPL.sl("adab_0", 40, 48)], False)
        wout_phase(S, C, 0, y, 0, wout, hin, hb, ntok)
        compute_mod(S, C, adaw, 0, 6, 3, C.pp[:, PPL.sl("adab_0", 48, 72)])
        mod_affine(S, C, C.pp[:, PPL.sl("g_ffn2_0")], True)
        ffn_phase(S, C, hb, hb, wg, wu, wd, ntok)
        if last:
            mod_setup_norm(S, C, adaw2, 0, C.pp[:, PPL.sl("finb")], C.pp[:, PPL.sl("g_fin")])
            final_phase(S, C, hb, hout, ntok)
        else:
            compute_mod(S, C, adaw2, 0, 0, 3, C.pp[:, PPL.sl("adab_1", 0, 24)])
            mod_affine(S, C, C.pp[:, PPL.sl("g_ffn1_1")], True)
            ffn_phase(S, C, hb, hout, wg2, wu2, wd2, ntok)
        S.emit()
    return nc


PAIRS = [[0, 1], [2, 3], [4, 5], [6, 7]]


class HT:
    def __init__(self, aps):
        self.aps = aps

    def __getitem__(self, key):
        rows, cols = key
        assert cols.start % 512 == 0 and cols.stop - cols.start == 512
        return self.aps[cols.start // 512][rows, :]

SSM_POS_G = (4, 5, 10, 11)


def wout_perm():
    a = np.arange(128)
    rows = []
    for r in range(2):
        rows += [(2 * r) * 128 + a, (2 * r + 1) * 128 + a, 512 + (2 * r) * 128 + a, 512 + (2 * r + 1) * 128 + a,
                 1024 + (2 * r) * 128 + a, 1024 + (2 * r + 1) * 128 + a]
    return np.concatenate(rows)


def build_fused():
    nc = bass.Bass("TRN2", target_bir_lowering=False)
    ext = lambda n, sh, dt=F32: nc.dram_tensor(n, sh, dt, kind="ExternalInput").ap()
    hin = ext("hin", [D, NTOK])
    pp_d = ext("pp", [128, PPL.n])
    cst_d = ext("cst", [128, CL.N])
    cstb_d = ext("cstb", [128, CL.NB16], BF16)
    rope_d = ext("rope", [2, 128, SEQ])
    W = []
    for l in range(2):
        W.append({k: ext("%s%d" % (k, l), sh) for k, sh in (
            ("adaw", [36, 128, 2048]), ("win", [D, WCOLS]), ("wout", [1536, D]),
            ("f1g", [11, 128, 2048]), ("f1u", [11, 128, 2048]), ("f1d", [8, 128, 2816]),
            ("f2g", [11, 128, 2048]), ("f2u", [11, 128, 2048]), ("f2d", [8, 128, 2816]))})
    fadaw = ext("fadaw", [8, 128, 2048])
    out = nc.dram_tensor("out", [D, NTOK], F32, kind="ExternalOutput").ap()
    hloc_t = [nc.dram_tensor("hloc%d" % i, [D, 512], F32) for i in range(8)]
    hg_t = [nc.dram_tensor("hg%d" % i, [2 * D, 512], F32) for i in range(8)]
    y_t = [nc.dram_tensor("yown%d" % i, [768, 512], BF16) for i in range(16)]
    yg_t = [nc.dram_tensor("yg%d" % i, [1536, 512], BF16) for i in range(16)]
    qs_d = nc.dram_tensor("qs_d", [256, SEQ], BF16).ap()
    kt_d = nc.dram_tensor("kt_d", [256, SEQ], BF16).ap()
    v_d = nc.dram_tensor("v_d", [SEQ, 256], BF16).ap()
    hloc = HT([t.ap() for t in hloc_t])
    hg = [t.ap() for t in hg_t]
    yo = [t.ap() for t in y_t]
    yg = [t.ap() for t in yg_t]

    def gather(S, src_t, dst_t):
        S.cc(lambda e: e.collective_compute("AllGather", ALU.bypass, replica_groups=PAIRS,
                                            ins=[src_t.ap().opt()], outs=[dst_t.ap().opt()]), "cc")

    hsrc_tile = lambda g: hg[g % 8][(g // 8) * D:(g // 8 + 1) * D, :]
    ydst_tile = lambda g: (yo[g], 0)
    with contextlib.ExitStack() as es:
        S = Sched(nc, es)
        C = Ctx()
        setup_common(S, C, pp_d)
        compute_mod(S, C, W[0]["adaw"], 0, 0, 3, C.pp[:, PPL.sl("adab_0", 0, 24)])
        mod_affine(S, C, C.pp[:, PPL.sl("g_ffn1_0")], True)
        ffn_phase(S, C, hin, hloc, W[0]["f1g"], W[0]["f1u"], W[0]["f1d"], NTOK)
        for l in range(2):
            w = W[l]
            for i in range(8):
                gather(S, hloc_t[i], hg_t[i])
            S.barrier()
            compute_mod(S, C, w["adaw"], 0, 3, 3, C.pp[:, PPL.sl("adab_%d" % l, 24, 48)])
            mod_affine(S, C, C.pp[:, PPL.sl("g_mix_%d" % l)], False)
            mixer1_phase(S, C, l, 0, hsrc_tile, w["win"], cst_d, cstb_d, rope_d, qs_d, kt_d, v_d, ydst_tile, YROW_OWN, 16)
            mixer2_phase(S, C, qs_d, kt_d, v_d, cst_d, cstb_d, ydst_tile, YROW_OWN, 16)
            for i in range(16):
                gather(S, y_t[i], yg_t[i])
            S.barrier()
            wout_phase(S, C, l, (lambda tt: yg[tt]), 0, w["wout"], hloc, hloc, NTOK,
                       y2_d=(lambda tt: yg[8 + tt]), ssm_pos=SSM_POS_G)
            compute_mod(S, C, w["adaw"], 0, 6, 3, C.pp[:, PPL.sl("adab_%d" % l, 48, 72)])
            mod_affine(S, C, C.pp[:, PPL.sl("g_ffn2_%d" % l)], True)
            ffn_phase(S, C, hloc, hloc, w["f2g"], w["f2u"], w["f2d"], NTOK)
            if l == 0:
                compute_mod(S, C, W[1]["adaw"], 0, 0, 3, C.pp[:, PPL.sl("adab_1", 0, 24)])
                mod_affine(S, C, C.pp[:, PPL.sl("g_ffn1_1")], True)
                ffn_phase(S, C, hloc, hloc, W[1]["f1g"], W[1]["f1u"], W[1]["f1d"], NTOK)
            else:
                mod_setup_norm(S, C, fadaw, 0, C.pp[:, PPL.sl("finb")], C.pp[:, PPL.sl("g_fin")])
                final_phase(S, C, hloc, out, NTOK)
        S.emit()
    return nc


_PROG = {}


def kernel(**inp):
    inp = {k: np.asarray(v) for k, v in inp.items()}
    f32 = lambda a: np.ascontiguousarray(a, dtype=np.float32)
    if "F" not in _PROG:
        _PROG["F"] = build_fused()
    nc = _PROG["F"]
    rope = make_rope()
    consts = [make_consts(hf) for hf in range(2)]
    perm = wout_perm()
    ab = lambda w: f32(np.asarray(w).reshape(8, 128, -1, 256).transpose(2, 1, 0, 3).reshape(-1, 128, 2048))
    shared = {"rope": rope, "fadaw": ab(inp["final_ada_w"])}
    gu = lambda w: f32(np.asarray(w).reshape(8, 128, 11, 256).transpose(2, 1, 0, 3).reshape(11, 128, 2048))
    dn = lambda w: f32(np.asarray(w).reshape(NFC, 128, 8, 128).transpose(2, 1, 0, 3).reshape(8, 128, 2816))
    for l in range(2):
        shared.update({"adaw%d" % l: ab(inp["ada_w"][l]), "wout%d" % l: f32(inp["w_out"][l][perm]),
                       "f1g%d" % l: gu(inp["ffn1_wg"][l]), "f1u%d" % l: gu(inp["ffn1_wu"][l]),
                       "f1d%d" % l: dn(inp["ffn1_wd"][l]), "f2g%d" % l: gu(inp["ffn2_wg"][l]),
                       "f2u%d" % l: gu(inp["ffn2_wu"][l]), "f2d%d" % l: dn(inp["ffn2_wd"][l])})
    wins = [[f32(inp["w_in"][l][:, win_cols(hf)]) for hf in range(2)] for l in range(2)]
    maps = []
    for b in range(NB):
        xT = f32(inp["x"][b].T)
        for hf in range(2):
            m = dict(shared)
            m.update({"hin": f32(xT[:, hf * NTOK:(hf + 1) * NTOK]), "pp": pack_pp(inp, b, hf, (0, 1)),
                      "cst": consts[hf][0], "cstb": consts[hf][1], "win0": wins[0][hf], "win1": wins[1][hf]})
            maps.append(m)
    res = run_bass_kernel_spmd(nc, maps, core_ids=list(range(8))).results
    o = np.empty((NB, SEQ, D), np.float32)
    for b in range(NB):
        o[b] = np.concatenate([res[2 * b]["out"], res[2 * b + 1]["out"]], axis=1).T
    return o
```
